# Optimizing a Trainium2 kernel written in Bass

```python
import jax
import jax.numpy as jnp
from jax import lax
import numpy as np

D_MODEL = 2048
BATCH = 2
SEQ = 16384
DEPTH = 2

GRID_W = 64
CTX_LEN = 256
HEAD_DIM = 128
A_Q_HEADS = 6
A_KV_HEADS = 2
A_WINDOW = 128
A_BLOCK = 128
B_HEADS = 6
B_WIN_H = 8
B_WIN_W = 16
C_CHANNELS = D_MODEL - (A_Q_HEADS + B_HEADS) * HEAD_DIM
C_CONV_WIDTH = 31
D_FF = 5632
ROPE_THETA = 10000.0
NORM_EPS = 1e-6
NEG_INF = -1e30
N_MOD = 9
A_Q_COLS = A_Q_HEADS * HEAD_DIM
A_KV_COLS = A_KV_HEADS * HEAD_DIM
B_COLS = B_HEADS * HEAD_DIM
IN_COLS = A_Q_COLS + 2 * A_KV_COLS + 3 * B_COLS + 2 * C_CHANNELS
MIX_WIDTH = A_Q_COLS + B_COLS + C_CHANNELS

kernel_name = "hymba_style_dit_window_natten_conformer"


def rms_norm(x, g):
    xf = x.astype(jnp.float32)
    y = xf * lax.rsqrt(jnp.mean(xf * xf, axis=-1, keepdims=True) + NORM_EPS)
    return (y * g.astype(jnp.float32)).astype(x.dtype)


def layer_norm(x, g, b):
    xf = x.astype(jnp.float32)
    mu = jnp.mean(xf, axis=-1, keepdims=True)
    xc = xf - mu
    y = xc * lax.rsqrt(jnp.mean(xc * xc, axis=-1, keepdims=True) + NORM_EPS)
    return (y * g.astype(jnp.float32) + b.astype(jnp.float32)).astype(x.dtype)


def modulate(x, g, shift, scale):
    return rms_norm(x, g) * (1 + scale) + shift


def swiglu(h, w_gate, w_up, w_down):
    return (jax.nn.silu(h @ w_gate) * (h @ w_up)) @ w_down


def axial_rope_tables(n_tokens):
    t = jnp.arange(n_tokens, dtype=jnp.int32)
    row = (t // GRID_W).astype(jnp.float32)
    col = (t % GRID_W).astype(jnp.float32)
    n_freq = HEAD_DIM // 4
    inv_freq = ROPE_THETA ** (-jnp.arange(n_freq, dtype=jnp.float32) / n_freq)
    ang_r = row[:, None] * inv_freq[None, :]
    ang_c = col[:, None] * inv_freq[None, :]
    return (jnp.cos(ang_r), jnp.sin(ang_r), jnp.cos(ang_c), jnp.sin(ang_c))


def _rotate(x, cos, sin):
    x1, x2 = jnp.split(x, 2, axis=-1)
    return jnp.concatenate([x1 * cos - x2 * sin, x2 * cos + x1 * sin], axis=-1)


def apply_axial_rope(x, tables):
    cos_r, sin_r, cos_c, sin_c = (t[None, :, None, :] for t in tables)
    xf = x.astype(jnp.float32)
    half = HEAD_DIM // 2
    out = jnp.concatenate([_rotate(xf[..., :half], cos_r, sin_r),
                           _rotate(xf[..., half:], cos_c, sin_c)], axis=-1)
    return out.astype(x.dtype)


def split_projection(p):
    b, n, _ = p.shape
    offs = list(np.cumsum([A_Q_COLS, A_KV_COLS, A_KV_COLS, B_COLS, B_COLS, B_COLS]))
    aq, ak, av, bq, bk, bv, cu = jnp.split(p, offs, axis=-1)
    heads = lambda t: t.reshape(b, n, -1, HEAD_DIM)
    return heads(aq), heads(ak), heads(av), heads(bq), heads(bk), heads(bv), cu


def context_attention(q, k, v, sink):
    b, n, hq, d = q.shape
    hkv = k.shape[2]
    g = hq // hkv
    qg = q.reshape(b, n, hkv, g, d)
    s = jnp.einsum('bqhgd,bkhd->bhgqk', qg, k).astype(jnp.float32) * (d ** -0.5)
    if sink is not None:
        s_sink = jnp.broadcast_to(sink.astype(jnp.float32).reshape(1, hkv, g, 1, 1), s.shape[:-1] + (1,))
        p = jax.nn.softmax(jnp.concatenate([s_sink, s], axis=-1), axis=-1)[..., 1:]
    else:
        p = jax.nn.softmax(s, axis=-1)
    o = jnp.einsum('bhgqk,bkhd->bqhgd', p.astype(v.dtype), v)
    return o.reshape(b, n, hq * d)


def windowed_gqa(q, k, v, kc, vc, sink):
    b, L, hq, d = q.shape
    hkv = k.shape[2]
    g = hq // hkv
    nb = L // A_BLOCK
    scale = d ** -0.5
    qb = q.reshape(b, nb, A_BLOCK, hkv, g, d)
    pad = ((0, 0), (A_BLOCK, A_BLOCK), (0, 0), (0, 0))
    kp = jnp.pad(k, pad).reshape(b, nb + 2, A_BLOCK, hkv, d)
    vp = jnp.pad(v, pad).reshape(b, nb + 2, A_BLOCK, hkv, d)
    kw = jnp.concatenate([kp[:, :-2], kp[:, 1:-1], kp[:, 2:]], axis=2)
    vw = jnp.concatenate([vp[:, :-2], vp[:, 1:-1], vp[:, 2:]], axis=2)
    rel = np.arange(3 * A_BLOCK)[None, :] - A_BLOCK - np.arange(A_BLOCK)[:, None]
    band = np.abs(rel) <= A_WINDOW
    kpos = (np.arange(nb)[:, None] - 1) * A_BLOCK + np.arange(3 * A_BLOCK)[None, :]
    inside = (kpos >= 0) & (kpos < L)
    mask = band[None, :, :] & inside[:, None, :]
    s_loc = jnp.einsum('bnqhgd,bnkhd->bhgnqk', qb, kw).astype(jnp.float32) * scale
    s_loc = jnp.where(mask, s_loc, NEG_INF)
    s_ctx = jnp.einsum('bnqhgd,bchd->bhgnqc', qb, kc).astype(jnp.float32) * scale
    s_sink = jnp.broadcast_to(sink.astype(jnp.float32).reshape(1, hkv, g, 1, 1, 1), s_ctx.shape[:-1] + (1,))
    p = jax.nn.softmax(jnp.concatenate([s_sink, s_loc, s_ctx], axis=-1), axis=-1)
    n_loc = 3 * A_BLOCK
    p_loc = p[..., 1:1 + n_loc].astype(v.dtype)
    p_ctx = p[..., 1 + n_loc:].astype(v.dtype)
    o = (jnp.einsum('bhgnqk,bnkhd->bnqhgd', p_loc, vw)
         + jnp.einsum('bhgnqc,bchd->bnqhgd', p_ctx, vc))
    return o.reshape(b, L, hq * d)


def neighbourhood_attention(q, k, v, kc, vc, rpb):
    b, L, h, d = q.shape
    rows = L // GRID_W
    kh = min(B_WIN_H, rows)
    kw = B_WIN_W
    scale = d ** -0.5
    col_start = np.clip(np.arange(GRID_W) - kw // 2, 0, GRID_W - kw)
    col_idx = col_start[:, None] + np.arange(kw)[None, :]
    col_bias_idx = col_idx - np.arange(GRID_W)[:, None] + (B_WIN_W - 1)
    qg = q.reshape(b, rows, GRID_W, h, d).transpose(1, 0, 3, 2, 4)
    kg = k.reshape(b, rows, GRID_W, h, d).transpose(0, 3, 1, 2, 4)
    vg = v.reshape(b, rows, GRID_W, h, d).transpose(0, 3, 1, 2, 4)
    kct = kc.transpose(0, 2, 1, 3)
    vct = vc.transpose(0, 2, 1, 3)
    n_loc = kh * kw

    def row_block(args):
        r, q_r = args
        rs = jnp.clip(r - kh // 2, 0, rows - kh)
        k_rows = lax.dynamic_slice_in_dim(kg, rs, kh, axis=2)
        v_rows = lax.dynamic_slice_in_dim(vg, rs, kh, axis=2)
        k_win = k_rows[:, :, :, col_idx, :]
        v_win = v_rows[:, :, :, col_idx, :]
        s_loc = jnp.einsum('bhjd,bhajcd->bhjac', q_r, k_win).astype(jnp.float32) * scale
        row_bias_idx = rs + jnp.arange(kh, dtype=jnp.int32) - r + (B_WIN_H - 1)
        bias = rpb[:, row_bias_idx][:, :, col_bias_idx]
        s_loc = s_loc + bias.transpose(0, 2, 1, 3).astype(jnp.float32)[None]
        s_loc = s_loc.reshape(b, h, GRID_W, n_loc)
        s_ctx = jnp.einsum('bhjd,bhcd->bhjc', q_r, kct).astype(jnp.float32) * scale
        p = jax.nn.softmax(jnp.concatenate([s_loc, s_ctx], axis=-1), axis=-1)
        p_loc = p[..., :n_loc].reshape(b, h, GRID_W, kh, kw).astype(v.dtype)
        p_ctx = p[..., n_loc:].astype(v.dtype)
        return (jnp.einsum('bhjac,bhajcd->bhjd', p_loc, v_win)
                + jnp.einsum('bhjc,bhcd->bhjd', p_ctx, vct))

    out = lax.map(row_block, (jnp.arange(rows, dtype=jnp.int32), qg))
    return out.transpose(1, 0, 3, 2, 4).reshape(b, L, h * d)


def conv_module(u, dw_w, dw_b, ln_g, ln_b):
    a, gt = jnp.split(u, 2, axis=-1)
    hh = a * jax.nn.sigmoid(gt)
    pad = (C_CONV_WIDTH - 1) // 2
    hh = lax.conv_general_dilated(hh, dw_w[:, None, :], window_strides=(1,), padding=[(pad, pad)],
                                  dimension_numbers=('NWC', 'WIO', 'NWC'),
                                  feature_group_count=C_CHANNELS) + dw_b
    return jax.nn.silu(layer_norm(hh, ln_g, ln_b))


def mixer(h, hc, w_in, w_out, a_q_norm, a_k_norm, a_sink, b_q_norm, b_k_norm, b_rpb,
          c_dw_w, c_dw_b, c_ln_g, c_ln_b, rope, with_ctx_out):
    aq, ak, av, bq, bk, bv, cu = split_projection(h @ w_in)
    caq, cak, cav, cbq, cbk, cbv, ccu = split_projection(hc @ w_in)
    aq = apply_axial_rope(rms_norm(aq, a_q_norm), rope)
    ak = apply_axial_rope(rms_norm(ak, a_k_norm), rope)
    bq = rms_norm(bq, b_q_norm)
    bk = rms_norm(bk, b_k_norm)
    cak = rms_norm(cak, a_k_norm)
    cbk = rms_norm(cbk, b_k_norm)
    o_a = windowed_gqa(aq, ak, av, cak, cav, a_sink)
    o_b = neighbourhood_attention(bq, bk, bv, cbk, cbv, b_rpb)
    o_c = conv_module(cu, c_dw_w, c_dw_b, c_ln_g, c_ln_b)
    y = jnp.concatenate([o_a, o_b, o_c], axis=-1) @ w_out
    if not with_ctx_out:
        return y, None
    co_a = context_attention(rms_norm(caq, a_q_norm), cak, cav, a_sink)
    co_b = context_attention(rms_norm(cbq, b_q_norm), cbk, cbv, None)
    co_c = conv_module(ccu, c_dw_w, c_dw_b, c_ln_g, c_ln_b)
    y_ctx = jnp.concatenate([co_a, co_b, co_c], axis=-1) @ w_out
    return y, y_ctx


def setup_inputs(seed: int = 0) -> dict:
    key = jax.random.key(seed)
    ks = jax.random.split(key, 32)
    f32 = jnp.float32
    D = D_MODEL

    def normal(k, shape, s):
        return jax.random.normal(k, shape, f32) * s

    def gain(k, shape):
        return 1.0 + 0.05 * jax.random.normal(k, shape, f32)

    return {
        "x": normal(ks[0], (BATCH, SEQ, D), 1.0),
        "c": normal(ks[1], (BATCH, D), 1.0),
        "ctx": normal(ks[2], (BATCH, CTX_LEN, D), 1.0),
        "c_ctx": normal(ks[3], (D,), 1.0),
        "w_mod": normal(ks[4], (DEPTH, D, N_MOD * D), 0.5 * D ** -0.5),
        "b_mod": normal(ks[5], (DEPTH, N_MOD * D), 0.02),
        "norm_ffn1": gain(ks[6], (DEPTH, D)),
        "norm_mix": gain(ks[7], (DEPTH, D)),
        "norm_ffn2": gain(ks[8], (DEPTH, D)),
        "ffn1_w_gate": normal(ks[9], (DEPTH, D, D_FF), D ** -0.5),
        "ffn1_w_up": normal(ks[10], (DEPTH, D, D_FF), D ** -0.5),
        "ffn1_w_down": normal(ks[11], (DEPTH, D_FF, D), D_FF ** -0.5),
        "ffn2_w_gate": normal(ks[12], (DEPTH, D, D_FF), D ** -0.5),
        "ffn2_w_up": normal(ks[13], (DEPTH, D, D_FF), D ** -0.5),
        "ffn2_w_down": normal(ks[14], (DEPTH, D_FF, D), D_FF ** -0.5),
        "w_in": normal(ks[15], (DEPTH, D, IN_COLS), D ** -0.5),
        "w_out": normal(ks[16], (DEPTH, MIX_WIDTH, D), MIX_WIDTH ** -0.5),
        "a_q_norm": gain(ks[17], (DEPTH, HEAD_DIM)),
        "a_k_norm": gain(ks[18], (DEPTH, HEAD_DIM)),
        "a_sink": normal(ks[19], (DEPTH, A_Q_HEADS), 0.5),
        "b_q_norm": gain(ks[20], (DEPTH, HEAD_DIM)),
        "b_k_norm": gain(ks[21], (DEPTH, HEAD_DIM)),
        "b_rpb": normal(ks[22], (DEPTH, B_HEADS, 2 * B_WIN_H - 1, 2 * B_WIN_W - 1), 0.1),
        "c_dw_w": normal(ks[23], (DEPTH, C_CONV_WIDTH, C_CHANNELS), C_CONV_WIDTH ** -0.5),
        "c_dw_b": normal(ks[24], (DEPTH, C_CHANNELS), 0.02),
        "c_ln_g": gain(ks[25], (DEPTH, C_CHANNELS)),
        "c_ln_b": normal(ks[26], (DEPTH, C_CHANNELS), 0.02),
    }


def reference(x, c, ctx, c_ctx, w_mod, b_mod, norm_ffn1, norm_mix, norm_ffn2,
              ffn1_w_gate, ffn1_w_up, ffn1_w_down, ffn2_w_gate, ffn2_w_up, ffn2_w_down,
              w_in, w_out, a_q_norm, a_k_norm, a_sink, b_q_norm, b_k_norm, b_rpb,
              c_dw_w, c_dw_b, c_ln_g, c_ln_b):
    rope = axial_rope_tables(x.shape[1])
    for l in range(DEPTH):
        last = l == DEPTH - 1
        m = jnp.split((jax.nn.silu(c) @ w_mod[l] + b_mod[l])[:, None, :], N_MOD, axis=-1)
        mc = jnp.split(jax.nn.silu(c_ctx) @ w_mod[l] + b_mod[l], N_MOD, axis=-1)
        x = x + 0.5 * m[2] * swiglu(modulate(x, norm_ffn1[l], m[0], m[1]),
                                    ffn1_w_gate[l], ffn1_w_up[l], ffn1_w_down[l])
        ctx = ctx + 0.5 * mc[2] * swiglu(modulate(ctx, norm_ffn1[l], mc[0], mc[1]),
                                         ffn1_w_gate[l], ffn1_w_up[l], ffn1_w_down[l])
        h = modulate(x, norm_mix[l], m[3], m[4])
        hc = modulate(ctx, norm_mix[l], mc[3], mc[4])
        y, y_ctx = mixer(h, hc, w_in[l], w_out[l], a_q_norm[l], a_k_norm[l], a_sink[l],
                         b_q_norm[l], b_k_norm[l], b_rpb[l], c_dw_w[l], c_dw_b[l],
                         c_ln_g[l], c_ln_b[l], rope, not last)
        x = x + m[5] * y
        x = x + 0.5 * m[8] * swiglu(modulate(x, norm_ffn2[l], m[6], m[7]),
                                    ffn2_w_gate[l], ffn2_w_up[l], ffn2_w_down[l])
        if not last:
            ctx = ctx + mc[5] * y_ctx
            ctx = ctx + 0.5 * mc[8] * swiglu(modulate(ctx, norm_ffn2[l], mc[6], mc[7]),
                                             ffn2_w_gate[l], ffn2_w_up[l], ffn2_w_down[l])
    return x
```

```python
import contextlib
import math
import numpy as np
import concourse.bass as bass
import concourse.mybir as mybir
from concourse.bass_utils import run_bass_kernel_spmd

F32 = mybir.dt.float32
BF16 = mybir.dt.bfloat16
AF = mybir.ActivationFunctionType
ALU = mybir.AluOpType

D = 2048
DFF = 5632
NCORE = 8
SEQ = 16384
OWN = 4096
HALO = 512
NEXT = OWN + 2 * HALO
NCH = NEXT // 128
CTX = 256
TT = 512
EPS = 1e-6
NEG = -30000.0
PADC = 16
IN_COLS = 4608
SEM_LIMIT = 30000


class SemObj:
    def __init__(self, h, dma, owner=None):
        self.h = h
        self.total = 0
        self.dma = dma
        self.owner = owner


class Buf:
    __slots__ = ("name", "w", "r")

    def __init__(self, name):
        self.name = name
        self.w = None
        self.r = {}


class Eng:
    def __init__(self, name, is_pe=False):
        self.name = name
        self.prog = []
        self.seen = {}
        self.sem = None
        self.is_pe = is_pe


class Pool:
    def __init__(self, items):
        self.items = items
        self.i = 0

    def get(self):
        it = self.items[self.i % len(self.items)]
        self.i += 1
        return it


class Stream:
    pass


class K:
    def __init__(self, nc, stack):
        self.nc = nc
        self.stack = stack
        self.pe = Eng("pe", True)
        self.act_e = Eng("act")
        self.dve = Eng("dve")
        self.pool = Eng("pool")
        self.sp = Eng("sp")
        self.nsem = 0

    def new_sem(self, dma, owner=None):
        self.nsem += 1
        h = self.stack.enter_context(self.nc.semaphore("s%d" % self.nsem))
        return SemObj(h, dma, owner)

    def sb(self, name, shape, dt):
        return self.stack.enter_context(self.nc.sbuf_tensor("sb_" + name, shape, dt))

    def _wait(self, eng, tickets):
        need = {}
        for t in tickets:
            if t is None:
                continue
            s, v = t
            if s.dma:
                v = s.total
            if eng.is_pe and s.owner is eng:
                continue
            if need.get(s, 0) < v:
                need[s] = v
        for s, v in need.items():
            if eng.seen.get(s, 0) < v:
                eng.prog.append(("w", s.h, v))
                eng.seen[s] = v

    def op(self, eng, fn, reads=(), writes=(), inc=True):
        tk = []
        for b in reads:
            tk.append(b.w)
        for b in writes:
            tk.append(b.w)
            tk.extend(b.r.items())
        self._wait(eng, tk)
        if inc:
            if eng.sem is None or eng.sem.total >= SEM_LIMIT:
                eng.sem = self.new_sem(False, eng)
            s = eng.sem
            s.total += 1
            eng.prog.append(("i", fn, s.h, 1))
            t = (s, s.total)
            for b in writes:
                b.w = t
                b.r = {}
            for b in reads:
                if b.r.get(s, 0) < s.total:
                    b.r[s] = s.total
        else:
            eng.prog.append(("i", fn, None, 0))

    def dma(self, q, out_ap, in_ap, sem, reads=(), writes=()):
        tk = [b.w for b in reads]
        for b in writes:
            if not (b.w is not None and b.w[0] is sem):
                tk.append(b.w)
            tk.extend(b.r.items())
        self._wait(q, tk)
        sem.total += 16
        q.prog.append(("i", (lambda e: e.dma_start(out=out_ap, in_=in_ap)), sem.h, 16))
        t = (sem, sem.total)
        for b in writes:
            b.w = t
            b.r = {}
        for b in reads:
            b.r[sem] = sem.total

    def mmgroup(self, outbuf, terms):
        allr = []
        for t in terms:
            for b in t[5]:
                if b not in allr:
                    allr.append(b)
        n = len(terms)
        for i, (o, l, r, st, sp_, bs) in enumerate(terms):
            last = i == n - 1
            fn = (lambda e, o=o, l=l, r=r, st=st, sp_=sp_: e.matmul(o, l, r, start=st, stop=sp_))
            self.op(self.pe, fn, reads=(allr if (last or i == 0) else ()), writes=[outbuf], inc=last)

    def act(self, out, in_, func, reads, writes, bias=None, scale=None):
        kw = {}
        if bias is not None:
            kw["bias"] = bias
        if scale is not None:
            kw["scale"] = scale
        self.op(self.act_e, (lambda e: e.activation(out=out, in_=in_, func=func, **kw)), reads, writes)

    def tt(self, out, in0, in1, op, reads, writes, eng=None):
        self.op(eng or self.dve, (lambda e: e.tensor_tensor(out=out, in0=in0, in1=in1, op=op)), reads, writes)

    def ts(self, out, in0, s1, s2, op0, op1, reads, writes, eng=None):
        if op1 is None:
            fn = (lambda e: e.tensor_scalar(out=out, in0=in0, scalar1=s1, scalar2=None, op0=op0))
        else:
            fn = (lambda e: e.tensor_scalar(out=out, in0=in0, scalar1=s1, scalar2=s2, op0=op0, op1=op1))
        self.op(eng or self.dve, fn, reads, writes)

    def stt(self, out, in0, scalar, in1, op0, op1, reads, writes, eng=None):
        self.op(eng or self.dve,
                (lambda e: e.scalar_tensor_tensor(out=out, in0=in0, scalar=scalar, in1=in1, op0=op0, op1=op1)),
                reads, writes)

    def recip(self, out, in_, reads, writes):
        self.op(self.dve, (lambda e: e.reciprocal(out=out, in_=in_)), reads, writes)

    def tcopy(self, out, in_, reads, writes, eng=None):
        self.op(eng or self.dve, (lambda e: e.tensor_copy(out=out, in_=in_)), reads, writes)

    def memset(self, ap, val, writes, eng=None):
        self.op(eng or self.dve, (lambda e: e.memset(ap, val)), (), writes)

    def wplan_set(self, gen):
        self.wgen = gen
        self.wissued = []
        self.wnext_i = 0
        self.wgen_done = False
        self.wcache = {}
        self.wcache_n = 0
        self.wpending = None
        self.wst_sem = self.new_sem(True)

    def _wissue_one(self):
        try:
            key, dmas, nel = next(self.wgen)
        except StopIteration:
            self.wgen_done = True
            if self.wpending is not None:
                self.wpending()
                self.wpending = None
            return
        i = len(self.wissued)
        s = i % 3
        cached = self.wcache.get(key) if nel else None
        if cached is not None:
            cap, cbuf = cached
            self.dma(self.pool, self.ws[s][:, 0:nel], cap, self.ws_sem[s], reads=[cbuf], writes=[self.wsb[s]])
        else:
            for (dst_fn, src) in dmas:
                self.dma(self.pool, dst_fn(self.ws[s]), src, self.ws_sem[s], reads=(), writes=[self.wsb[s]])
        if self.wpending is not None:
            self.wpending()
            self.wpending = None
        if nel and cached is None:
            self.wcache_n += 1
            cap = self.nc.dram_tensor("wc%d" % self.wcache_n, [128, nel], BF16, kind="Internal").ap()
            cbuf = Buf("wc%d" % self.wcache_n)
            self.wcache[key] = (cap, cbuf)

            def _store(s=s, cap=cap, cbuf=cbuf, nel=nel):
                self.dma(self.pool, cap, self.ws[s][:, 0:nel], self.wst_sem, reads=[self.wsb[s]], writes=[cbuf])
            self.wpending = _store
        self.wissued.append(key)

    def wnext(self, key):
        i = self.wnext_i
        while len(self.wissued) < i + 3 and not self.wgen_done:
            self._wissue_one()
        assert self.wissued[i] == key, (self.wissued[i], key)
        self.wnext_i += 1
        s = i % 3
        return self.ws[s], self.wsb[s]


def _v3(ap2d, c):
    return ap2d.rearrange("p (k c) -> p k c", c=c)


def build_nc(debug=False):
    nc = bass.Bass("TRN2", target_bir_lowering=False)
    stack = contextlib.ExitStack()
    with stack:
        _build(nc, stack, debug)
    return nc


def _build(nc, stack, debug):
    k = K(nc, stack)

    def din(name, shape, dt=F32):
        return nc.dram_tensor(name, list(shape), dt, kind="ExternalInput").ap()

    def dscr(name, shape, dt):
        return nc.dram_tensor(name, list(shape), dt, kind=("ExternalOutput" if debug else "Internal")).ap()

    xT_d = din("xT", [D, NEXT])
    ctxT_d = din("ctxT", [D, CTX])
    cm_d = din("cm", [128, 16, 2])
    wmod_d = din("w_mod", [2, D, 9 * D])
    bmod_d = din("b_modT", [2, 128, 144])
    gn_d = din("gnT", [128, 2, 3, 16])
    wg_d = [din("ffn1_w_gate", [2, D, DFF]), din("ffn2_w_gate", [2, D, DFF])]
    wu_d = [din("ffn1_w_up", [2, D, DFF]), din("ffn2_w_up", [2, D, DFF])]
    wd_d = [din("ffn1_w_down", [2, DFF, D]), din("ffn2_w_down", [2, DFF, D])]
    win_d = din("w_in", [2, D, IN_COLS])
    wout_d = din("w_out", [2, D, D])
    qkg_d = din("qkg", [128, 2, 4])
    sink_d = din("sinkT", [128, 2, 6])
    biasA_d = din("biasA", [128, 3, 3, 128])
    biasB_d = din("biasB", [2, 5, 6, 128, 7, 128])
    cos_d = din("cosT", [128, NEXT])
    sin_d = din("sinT", [128, NEXT])
    valid_d = din("validT", [128, NEXT])
    dw_d = din("dwT", [128, 2, 4, 31])
    cp_d = din("cpT", [128, 2, 3, 4])
    consts_d = din("consts", [128, 3, 128])
    out_d = nc.dram_tensor("outT", [D, OWN], F32, kind="ExternalOutput").ap()

    x1_s = [dscr("x1s%d" % l, [D, NEXT], F32) for l in range(2)]
    QT_s = [dscr("QTs%d" % l, [12, 128, NEXT], BF16) for l in range(2)]
    KT_s = [dscr("KTs%d" % l, [8, 128, NEXT], BF16) for l in range(2)]
    VA_s = [dscr("VAs%d" % l, [NEXT, 256], BF16) for l in range(2)]
    VB_s = [dscr("VBs%d" % l, [NEXT, 768], BF16) for l in range(2)]
    HH_s = [dscr("HHs%d" % l, [4, 128, NEXT + 2 * PADC], BF16) for l in range(2)]
    cx1_s = dscr("cx1s", [D, CTX], F32)
    cQT_s = dscr("cQTs", [12, 128, CTX], BF16)
    cHH_s = dscr("cHHs", [4, 128, CTX + 2 * PADC], BF16)

    xT = k.sb("xT", [128, 16, TT], F32)
    xn = k.sb("xn", [128, 16, TT], BF16)
    hT = k.sb("hT", [128, 22, TT], BF16)
    k.ws = [k.sb("ws%d" % i, [128, 8192], BF16) for i in range(3)]
    k.wsb = [Buf("ws%d" % i) for i in range(3)]
    k.ws_sem = [k.new_sem(True) for _ in range(3)]
    xTb = [Buf("xT%d" % i) for i in range(16)]
    xnb = [Buf("xn%d" % i) for i in range(16)]
    hTb = [Buf("hT%d" % i) for i in range(22)]
    ps_t = stack.enter_context(nc.psum_tensor("ps", [128, 8, 512], F32))
    psb = [Buf("ps%d" % i) for i in range(8)]

    def mkpool(name, n, shape, dt):
        items = []
        for i in range(n):
            items.append((k.sb("%s%d" % (name, i), shape, dt), Buf("%s%d" % (name, i))))
        return Pool(items)

    rs_pool = mkpool("rs", 3, [128, TT], F32)
    tf_pool = mkpool("tf", 4, [128, TT], F32)
    tb_pool = mkpool("tb", 6, [128, TT], BF16)
    vst_pool = mkpool("vst", 2, [128, 512], BF16)
    qT_pool = mkpool("qTh", 2, [128, TT], BF16)
    kT_pool = mkpool("kTh", 2, [128, 10 * 128], BF16)
    vA_all = k.sb("vA_all", [128, 6, 256], BF16)
    vB_all = k.sb("vB_all", [128, 9, 768], BF16)
    vAb, vBb = Buf("vA_all"), Buf("vB_all")
    vA_sem, vB_sem = k.new_sem(True), k.new_sem(True)
    bB_pool = mkpool("bB", 2, [128, 6 * 128], BF16)
    pT_pool = mkpool("pT", 2, [128, 8 * 128], BF16)
    q_sems = [k.new_sem(True) for _ in range(2)]
    k_sems = [k.new_sem(True) for _ in range(2)]
    bB_sem = [k.new_sem(True), k.new_sem(True)]
    ckT = k.sb("ckT", [128, 8, CTX], BF16)
    cv = k.sb("cv", [128, 2, 8, 128], BF16)
    ckTb = Buf("ckT")
    cvb = Buf("cv")
    hhc = k.sb("hhc", [128, 4, TT + 30], BF16)
    hhcb = Buf("hhc")
    hhc_sem = k.new_sem(True)
    acc = [k.sb("acc%d" % i, [128, TT], F32) for i in range(4)]
    accb = [Buf("acc%d" % i) for i in range(4)]
    cst = k.sb("cst", [128, 3, 128], BF16)
    cstb = Buf("cst")
    cosS = k.sb("cosS", [128, TT], F32)
    sinS = k.sb("sinS", [128, TT], F32)
    valS = k.sb("valS", [128, TT], F32)
    cosb, sinb, valb = Buf("cos"), Buf("sin"), Buf("val")
    tab_sem = k.new_sem(True)
    cmS = k.sb("cmS", [128, 16, 2], F32)
    scS = k.sb("scS", [128, 16, 2], BF16)
    modT = [k.sb("modT%d" % l, [128, 144, 2], F32) for l in range(2)]
    der = [k.sb("der%d" % l, [128, 2, 9, 16], F32) for l in range(2)]
    bmodS = k.sb("bmodS", [128, 2, 144], F32)
    gnS = k.sb("gnS", [128, 2, 3, 16], F32)
    qkgS = k.sb("qkgS", [128, 2, 4], F32)
    sinkS = k.sb("sinkS", [128, 2, 6], F32)
    bAS = k.sb("bAS", [128, 3, 3, 128], BF16)
    dwS = k.sb("dwS", [128, 2, 4, 31], F32)
    cpS = k.sb("cpS", [128, 2, 3, 4], F32)
    zer = k.sb("zer", [128, PADC], BF16)
    prmb = Buf("params")
    modb = [Buf("mod0"), Buf("mod1")]
    derb = [Buf("der0"), Buf("der1")]
    prm_sem = k.new_sem(True)
    x_sem = k.new_sem(True)
    st_sem = k.new_sem(True)
    out_sem = k.new_sem(True)
    scrb = {}

    def sbuf_of(name):
        if name not in scrb:
            scrb[name] = Buf(name)
        return scrb[name]

    ones = cst[:, 0, :]
    ident = cst[:, 1, :]
    perm = cst[:, 2, :]

    def PS(i):
        return ps_t[:, i, :]

    lat = Stream()
    lat.is_ctx = False
    lat.col = 0
    ctxs = Stream()
    ctxs.is_ctx = True
    ctxs.col = 1

    def w_mod_loads(L):
        for g in range(36):
            src = wmod_d[L, :, g * 512:(g + 1) * 512].rearrange("(k p) c -> p k c", p=128)
            yield (("mod", L, g), [((lambda w: _v3(w[:, 0:8192], 512)), src)], 0)

    plan_mod_i = [0]

    def w_ffn_loads(L, which, mod_inter=False):
        wg, wu, wd = wg_d[which][L], wu_d[which][L], wd_d[which][L]
        for hf in range(2):
            for gi in range(11):
                if mod_inter and gi % 4 == 3 and plan_mod_i[0] < 36:
                    g_ = plan_mod_i[0]
                    plan_mod_i[0] += 1
                    src_ = wmod_d[1, :, g_ * 512:(g_ + 1) * 512].rearrange("(k p) c -> p k c", p=128)
                    yield (("mod", 1, g_), [((lambda w: _v3(w[:, 0:8192], 512)), src_)], 0)
                c0 = hf * 2816 + gi * 256
                sg = wg[:, c0:c0 + 256].rearrange("(k p) c -> p k c", p=128)
                su = wu[:, c0:c0 + 256].rearrange("(k p) c -> p k c", p=128)
                yield (("gu", L, which, hf, gi), [((lambda w: _v3(w[:, 0:4096], 256)), sg),
                                                  ((lambda w: _v3(w[:, 4096:8192], 256)), su)], 8192)
            for oi in range(8):
                sd = wd[hf * 2816:(hf + 1) * 2816, oi * 256:(oi + 1) * 256].rearrange("(k p) c -> p k c", p=128)
                yield (("dn", L, which, hf, oi), [((lambda w: _v3(w[:, 0:5632], 256)), sd)], 5632)

    WIN_GROUPS = [[0, 1, 2, 3], [4, 5, 6, 7], [8, 9, 10, 11], [12, 13, 14, 15], [16, 17, 18, 19],
                  [20, 21, 22, 23], [24, 25, 26, 27], [28, 29, 32, 33], [30, 31, 34, 35]]

    def w_in_loads(L, groups):
        for gi in groups:
            blks = WIN_GROUPS[gi]
            if gi < 7:
                c0 = blks[0] * 128
                src = win_d[L, :, c0:c0 + 512].rearrange("(k p) c -> p k c", p=128)
                yield (("win", L, gi), [((lambda w: _v3(w[:, 0:8192], 512)), src)], 8192)
            else:
                ca = blks[0] * 128
                cg = blks[2] * 128
                s1 = win_d[L, :, ca:ca + 256].rearrange("(k p) c -> p k c", p=128)
                s2 = win_d[L, :, cg:cg + 256].rearrange("(k p) c -> p k c", p=128)
                yield (("win", L, gi), [((lambda w: _v3(w[:, 0:8192], 512)[:, :, 0:256]), s1),
                                        ((lambda w: _v3(w[:, 0:8192], 512)[:, :, 256:512]), s2)], 8192)

    def w_out_loads(L):
        for g in range(4):
            src = wout_d[L, :, g * 512:(g + 1) * 512].rearrange("(k p) c -> p k c", p=128)
            yield (("wout", L, g), [((lambda w: _v3(w[:, 0:8192], 512)), src)], 8192)

    ALLG = list(range(9))
    KVG = [1, 2, 4, 5, 6]
    NT1 = NEXT // TT
    NT2 = (OWN + 512) // TT
    NT3 = OWN // TT

    def plan():
        yield from w_mod_loads(0)
        for i_ in range(NT1):
            yield from w_ffn_loads(0, 0, mod_inter=True)
            yield from w_in_loads(0, ALLG)
            if i_ == 0:
                yield from w_ffn_loads(0, 0)
                yield from w_in_loads(0, ALLG)
        for i_ in range(NT2):
            yield from w_out_loads(0)
            yield from w_ffn_loads(0, 1)
            yield from w_ffn_loads(1, 0)
            yield from w_in_loads(1, ALLG)
            if i_ == 0:
                yield from w_out_loads(0)
                yield from w_ffn_loads(0, 1)
                yield from w_ffn_loads(1, 0)
        yield from w_in_loads(1, KVG)
        for _ in range(NT3):
            yield from w_out_loads(1)
            yield from w_ffn_loads(1, 1)

    k.wplan_set(plan())

    def pload(dst, src, q=None):
        k.dma(q or k.sp, dst, src, prm_sem, reads=(), writes=[prmb])

    pload(cst[:], consts_d, q=k.pool)
    pload(bAS[:], biasA_d, q=k.pool)
    pload(cmS[:], cm_d)
    pload(bmodS[:, 0, :], bmod_d[0])
    pload(bmodS[:, 1, :], bmod_d[1])
    pload(gnS[:], gn_d)
    pload(qkgS[:], qkg_d)
    pload(sinkS[:], sink_d)
    pload(dwS[:], dw_d)
    pload(cpS[:], cp_d)
    esink = k.sb("esink", [128, 2, 6], F32)
    qkgE = k.sb("qkgE", [128, 2, 4], F32)
    prm2b = Buf("params2")
    esinkb = Buf("esink")
    scSb = Buf("scS")
    zerb = Buf("zer")
    k.act(esink[:], sinkS[:], AF.Exp, [prmb], [esinkb])
    k.act(scS[:], cmS[:], AF.Silu, [prmb], [scSb])
    k.ts(qkgE[:], qkgS[:], 1.0, None, ALU.mult, None, [prmb], [prm2b])
    for l in range(2):
        for j in (0, 2):
            k.ts(qkgE[:, l, j:j + 1], qkgS[:, l, j:j + 1], 128.0 ** -0.5, None, ALU.mult, None, [prmb, prm2b], [prm2b])
    k.memset(zer[:], 0.0, [zerb])
    for hs, n in ((HH_s[0], NEXT), (HH_s[1], NEXT), (cHH_s, CTX)):
        for c in range(4):
            k.dma(k.sp, hs[c, :, 0:PADC], zer[:], st_sem, reads=[zerb], writes=[sbuf_of("pads")])
            k.dma(k.sp, hs[c, :, PADC + n:PADC + n + PADC], zer[:], st_sem, reads=[zerb], writes=[sbuf_of("pads")])

    def mod_slot(L, g):
        pb = psb[7]
        w, wb = k.wnext(("mod", L, g))
        wv = _v3(w[:, 0:8192], 512)
        for s in range(4):
            terms = []
            for kk in range(16):
                terms.append((PS(7)[:, 2 * s:2 * s + 2], wv[:, kk, s * 128:(s + 1) * 128], scS[:, kk, :],
                              kk == 0, kk == 15, [wb, scSb]))
            k.mmgroup(pb, terms)
        pv = PS(7)[:, 0:8].rearrange("p (j c) -> p j c", c=2)
        for col in range(2):
            k.tt(modT[L][:, 4 * g:4 * g + 4, col], pv[:, :, col], bmodS[:, L, 4 * g:4 * g + 4], ALU.add,
                 [pb, prmb], [modb[L]])

    def mod_step(L):
        for g in range(36):
            mod_slot(L, g)
        mod_derive(L)

    def mod_derive(L):
        for col in range(2):
            for i in range(9):
                src = modT[L][:, i * 16:(i + 1) * 16, col]
                dst = der[L][:, col, i, :]
                if i in (0, 3, 6):
                    k.tcopy(dst, src, [modb[L]], [derb[L]])
                elif i in (1, 4, 7):
                    k.stt(dst, src, 1.0, gnS[:, L, i // 3, :], ALU.add, ALU.mult, [modb[L], prmb], [derb[L]])
                elif i in (2, 8):
                    k.ts(dst, src, 0.5, None, ALU.mult, None, [modb[L]], [derb[L]])
                else:
                    k.tcopy(dst, src, [modb[L]], [derb[L]])

    def norm_step(S, L, T, which, presq=None):
        a_ap = der[L][:, S.col, which * 3 + 1, :]
        sh_ap = der[L][:, S.col, which * 3 + 0, :]
        if presq is None:
            for kk in range(16):
                k.act(hT[:, kk, :T], xT[:, kk, :T], AF.Square, [xTb[kk]], [hTb[kk]])
        if presq == "xn":
            terms = [(PS(6)[:, :T], ones, xn[:, kk, :T], kk == 0, kk == 15, [xnb[kk], prm2b]) for kk in range(16)]
        else:
            terms = [(PS(6)[:, :T], ones, hT[:, kk, :T], kk == 0, kk == 15, [hTb[kk], prm2b]) for kk in range(16)]
        k.mmgroup(psb[6], terms)
        sq_, sqb_ = tf_pool.get()
        k.act(sq_[:, :T], PS(6)[:, :T], AF.Ln, [psb[6]], [sqb_], bias=EPS, scale=1.0 / 2048.0)
        r, rb = rs_pool.get()
        k.act(r[:, :T], sq_[:, :T], AF.Exp, [sqb_], [rb], scale=-0.5)
        for kk in range(16):
            t, tb = tf_pool.get()
            k.tt(t[:, :T], xT[:, kk, :T], r[:, :T], ALU.mult, [xTb[kk], rb], [tb])
            k.act(xn[:, kk, :T], t[:, :T], AF.Identity, [tb, derb[L]], [xnb[kk]],
                  bias=sh_ap[:, kk:kk + 1], scale=a_ap[:, kk:kk + 1])

    gu_ctr = [0]
    dn_ctr = [0]

    prog_mod_i = [0]

    def ffn_step(S, L, which, T, sq_hook=False, mod_inter=False):
        coef = der[L][:, S.col, (0 if which == 0 else 2) * 3 + 2, :]
        for hf in range(2):
            for gi in range(11):
                if mod_inter and gi % 4 == 3 and prog_mod_i[0] < 36:
                    mod_slot(1, prog_mod_i[0])
                    prog_mod_i[0] += 1
                    if prog_mod_i[0] == 36:
                        mod_derive(1)
                w, wb = k.wnext(("gu", L, which, hf, gi))
                wgv = _v3(w[:, 0:4096], 256)
                wuv = _v3(w[:, 4096:8192], 256)
                for s in range(2):
                    j = gi * 2 + s
                    pg = (gu_ctr[0] % 2) * 2
                    gu_ctr[0] += 1
                    tg = [(PS(pg)[:, :T], wgv[:, kk, s * 128:(s + 1) * 128], xn[:, kk, :T], kk == 0, kk == 15,
                           [wb, xnb[kk]]) for kk in range(16)]
                    k.mmgroup(psb[pg], tg)
                    tu = [(PS(pg + 1)[:, :T], wuv[:, kk, s * 128:(s + 1) * 128], xn[:, kk, :T], kk == 0, kk == 15,
                           [wb, xnb[kk]]) for kk in range(16)]
                    k.mmgroup(psb[pg + 1], tu)
                    sg, sgb = tf_pool.get()
                    k.act(sg[:, :T], PS(pg)[:, :T], AF.Silu, [psb[pg]], [sgb])
                    k.tt(hT[:, j, :T], PS(pg + 1)[:, :T], sg[:, :T], ALU.mult, [psb[pg + 1], sgb], [hTb[j]])
            for oi in range(8):
                w, wb = k.wnext(("dn", L, which, hf, oi))
                wdv = _v3(w[:, 0:5632], 256)
                for s in range(2):
                    o = oi * 2 + s
                    pd = 4 + (dn_ctr[0] % 2)
                    dn_ctr[0] += 1
                    td = [(PS(pd)[:, :T], wdv[:, kk, s * 128:(s + 1) * 128], hT[:, kk, :T], kk == 0, kk == 21,
                           [wb, hTb[kk]]) for kk in range(22)]
                    k.mmgroup(psb[pd], td)
                    k.stt(xT[:, o, :T], PS(pd)[:, :T], coef[:, o:o + 1], xT[:, o, :T], ALU.mult, ALU.add,
                          [psb[pd], xTb[o], derb[L]], [xTb[o]])
                    if sq_hook and hf == 1:
                        k.act(xn[:, o, :T], xT[:, o, :T], AF.Square, [xTb[o]], [xnb[o]])

    win_ctr = [0]

    def win_step(S, L, t0, T, groups):
        if not S.is_ctx:
            k.dma(k.sp, cosS[:, :T], cos_d[:, t0:t0 + T], tab_sem, (), [cosb])
            k.dma(k.sp, sinS[:, :T], sin_d[:, t0:t0 + T], tab_sem, (), [sinb])
            k.dma(k.sp, valS[:, :T], valid_d[:, t0:t0 + T], tab_sem, (), [valb])
        QT = cQT_s if S.is_ctx else QT_s[L]
        HH = cHH_s if S.is_ctx else HH_s[L]
        qn_name = "cQT" if S.is_ctx else "QT%d" % L
        ntb = T // 128

        def feat_block(wv, ci):
            pa = win_ctr[0] % 4
            win_ctr[0] += 1
            terms = [(PS(pa)[:, :T], wv[:, kk, ci * 128:(ci + 1) * 128], xn[:, kk, :T], kk == 0, kk == 15,
                      [wcur[1], xnb[kk]]) for kk in range(16)]
            k.mmgroup(psb[pa], terms)
            return pa

        def qk_post(pa, gidx, rope, out_dram, out_sb, out_sb_buf, dname):
            sq, sqb = tb_pool.get()
            k.act(sq[:, :T], PS(pa)[:, :T], AF.Square, [psb[pa]], [sqb])
            k.mmgroup(psb[6], [(PS(6)[:, :T], ones, sq[:, :T], True, True, [sqb, prm2b])])
            sq_, sqb_ = tf_pool.get()
            k.act(sq_[:, :T], PS(6)[:, :T], AF.Ln, [psb[6]], [sqb_], bias=EPS, scale=1.0 / 128.0)
            r, rb = rs_pool.get()
            k.act(r[:, :T], sq_[:, :T], AF.Exp, [sqb_], [rb], scale=-0.5)
            g_ap = qkgE[:, L, gidx:gidx + 1]
            if not rope:
                if out_sb is not None:
                    k.stt(out_sb, PS(pa)[:, :T], g_ap, r[:, :T], ALU.mult, ALU.mult, [psb[pa], rb, prm2b], [out_sb_buf])
                else:
                    qn, qnb = tb_pool.get()
                    k.stt(qn[:, :T], PS(pa)[:, :T], g_ap, r[:, :T], ALU.mult, ALU.mult, [psb[pa], rb, prm2b], [qnb])
                    k.dma(k.sp, out_dram, qn[:, :T], st_sem, [qnb], [sbuf_of(dname)])
                return
            qn, qnb = tb_pool.get()
            k.stt(qn[:, :T], PS(pa)[:, :T], g_ap, r[:, :T], ALU.mult, ALU.mult, [psb[pa], rb, prm2b], [qnb])
            k.mmgroup(psb[7], [(PS(7)[:, :T], perm, qn[:, :T], True, True, [qnb, prm2b])])
            t1, t1b = tf_pool.get()
            k.tt(t1[:, :T], qn[:, :T], cosS[:, :T], ALU.mult, [qnb, cosb], [t1b])
            t2, t2b = tf_pool.get()
            k.tt(t2[:, :T], PS(7)[:, :T], sinS[:, :T], ALU.mult, [psb[7], sinb], [t2b])
            qo, qob = tb_pool.get()
            k.tt(qo[:, :T], t1[:, :T], t2[:, :T], ALU.add, [t1b, t2b], [qob])
            k.dma(k.sp, out_dram, qo[:, :T], st_sem, [qob], [sbuf_of(dname)])

        def v_run(wv, ci0, n, vdram, vcol0, kvslot0, dname):
            for tb_i in range(ntb):
                nn = n * 128
                done = 0
                while done < nn:
                    w_ = min(512, nn - done)
                    pv = 4 + (win_ctr[0] % 2)
                    win_ctr[0] += 1
                    terms = [(PS(pv)[:, :w_], xn[:, kk, tb_i * 128:(tb_i + 1) * 128],
                              wv[:, kk, ci0 * 128 + done:ci0 * 128 + done + w_], kk == 0, kk == 15,
                              [wcur[1], xnb[kk]]) for kk in range(16)]
                    k.mmgroup(psb[pv], terms)
                    if S.is_ctx:
                        nb_ = w_ // 128
                        s0 = kvslot0 + done // 128
                        k.act(cv[:, tb_i, s0:s0 + nb_, :], PS(pv)[:, :w_].rearrange("p (s d) -> p s d", d=128),
                              AF.Copy, [psb[pv]], [cvb])
                    else:
                        vs, vsb = vst_pool.get()
                        k.act(vs[:, :w_], PS(pv)[:, :w_], AF.Copy, [psb[pv]], [vsb])
                        k.dma(k.sp, vdram[t0 + tb_i * 128:t0 + (tb_i + 1) * 128, vcol0 + done:vcol0 + done + w_],
                              vs[:, :w_], st_sem, [vsb], [sbuf_of(dname)])
                    done += w_

        pending = []

        def flush():
            while pending:
                pending.pop(0)()

        for gi in groups:
            w, wb = k.wnext(("win", L, gi))
            wcur = (w, wb)
            wv = _v3(w[:, 0:8192], 512)
            blks = WIN_GROUPS[gi]
            if gi >= 7:
                flush()
                need_cu = not (S.is_ctx and L == 1)
                if not need_cu:
                    continue
                for i in range(2):
                    cc = (gi - 7) * 2 + i
                    pa_a = feat_block(wv, i)
                    pa_g = feat_block(wv, 2 + i)
                    sg, sgb = tf_pool.get()
                    k.act(sg[:, :T], PS(pa_g)[:, :T], AF.Sigmoid, [psb[pa_g]], [sgb])
                    ho, hob = tb_pool.get()
                    if S.is_ctx:
                        k.tt(ho[:, :T], PS(pa_a)[:, :T], sg[:, :T], ALU.mult, [psb[pa_a], sgb], [hob])
                    else:
                        s2, s2b = tf_pool.get()
                        k.tt(s2[:, :T], sg[:, :T], valS[:, :T], ALU.mult, [sgb, valb], [s2b])
                        k.tt(ho[:, :T], PS(pa_a)[:, :T], s2[:, :T], ALU.mult, [psb[pa_a], s2b], [hob])
                    k.dma(k.sp, HH[cc, :, PADC + t0:PADC + t0 + T], ho[:, :T], st_sem, [hob],
                          [sbuf_of("cHH" if S.is_ctx else "HH%d" % L)])
                continue
            ci = 0
            while ci < 4:
                b = blks[ci]
                ctx_kv_only = S.is_ctx and L == 1
                if b < 6 or 10 <= b < 16:
                    if ctx_kv_only:
                        ci += 1
                        continue
                    isA = b < 6
                    qi = b if isA else 6 + (b - 10)
                    pa = feat_block(wv, ci)
                    flush()
                    pending.append(lambda pa=pa, isA=isA, qi=qi: qk_post(pa, 0 if isA else 2, isA and not S.is_ctx,
                                                                        QT[qi, :, t0:t0 + T], None, None, qn_name))
                    ci += 1
                elif b in (6, 7) or 16 <= b < 22:
                    isA = b < 8
                    ki = (b - 6) if isA else 2 + (b - 16)
                    pa = feat_block(wv, ci)
                    flush()
                    if S.is_ctx:
                        pending.append(lambda pa=pa, isA=isA, ki=ki: qk_post(pa, 1 if isA else 3, False, None,
                                                                            ckT[:, ki, :T], ckTb, None))
                    else:
                        pending.append(lambda pa=pa, isA=isA, ki=ki: qk_post(pa, 1 if isA else 3, isA,
                                                                            KT_s[L][ki, :, t0:t0 + T], None, None,
                                                                            "KT%d" % L))
                    ci += 1
                else:
                    n = 0
                    while ci + n < 4 and (blks[ci + n] in (8, 9) or 22 <= blks[ci + n] < 28):
                        n += 1
                    isA = b < 10
                    if isA:
                        v_run(wv, ci, n, VA_s[L], (b - 8) * 128, (b - 8), "VA%d" % L)
                    else:
                        v_run(wv, ci, n, VB_s[L], (b - 22) * 128, 2 + (b - 22), "VB%d" % L)
                    ci += n
        flush()

    att_ctr = [0]
    head_ctr = [0]
    EDGE_TOP = (4, 5)
    EDGE_BOT = (34, 35)

    def offs_of(g, isA_):
        if isA_:
            return [-1, 0, 1]
        if g in EDGE_TOP:
            return [-2, -1, 0, 1, 2, 3]
        if g in EDGE_BOT:
            return [-3, -2, -1, 0, 1, 2]
        return [-2, -1, 0, 1, 2]

    pref = {}

    def attn_prefetch(S, L, t0, T):
        key = (S.is_ctx, L, t0)
        if key in pref:
            return pref[key]
        G = T // 128
        g0 = t0 // 128
        conv_prep(S, L, t0, T)
        rng = {}
        if not S.is_ctx:
            for isA_ in (True, False):
                lo_ = min(g0 + gi_ + offs_of(g0 + gi_, isA_)[0] for gi_ in range(G))
                hi_ = max(g0 + gi_ + offs_of(g0 + gi_, isA_)[-1] for gi_ in range(G)) + 1
                lo_, hi_ = max(0, lo_), min(NCH, hi_)
                rng[isA_] = (lo_, hi_)
            la, ha = rng[True]
            k.dma(k.sp, vA_all[:, :ha - la, :], VA_s[L][la * 128:ha * 128, :].rearrange("(c p) d -> p c d", p=128),
                  vA_sem, [sbuf_of("VA%d" % L)], [vAb])
            lb, hb = rng[False]
            assert hb - lb <= 9 and ha - la <= 6
            k.dma(k.sp, vB_all[:, :hb - lb, :], VB_s[L][lb * 128:hb * 128, :].rearrange("(c p) d -> p c d", p=128),
                  vB_sem, [sbuf_of("VB%d" % L)], [vBb])
        pref[key] = rng
        return rng

    def attn_step(S, L, t0, T, after_head=None):
        G = T // 128
        g0 = t0 // 128
        rng = attn_prefetch(S, L, t0, T)
        QT = cQT_s if S.is_ctx else QT_s[L]
        qn_name = "cQT" if S.is_ctx else "QT%d" % L
        for hd in range(12):
            isA = hd < 6
            hh = hd if isA else hd - 6
            kvslot = (hh // 3) if isA else 2 + hh
            q_sem = q_sems[qT_pool.i % 2]
            q, qb = qT_pool.get()
            k.dma(k.sp, q[:, :T], QT[hd, :, t0:t0 + T], q_sem, [sbuf_of(qn_name)], [qb])
            if not S.is_ctx:
                c_lo, c_hi = rng[isA]
                nch = c_hi - c_lo
                k_sem = k_sems[kT_pool.i % 2]
                kt, ktb = kT_pool.get()
                k.dma(k.sp, kt[:, :nch * 128], KT_s[L][kvslot, :, c_lo * 128:c_hi * 128], k_sem,
                      [sbuf_of("KT%d" % L)], [ktb])
            po = 4 + (head_ctr[0] % 2)
            pdn = 6 + (head_ctr[0] % 2)
            head_ctr[0] += 1
            pend_pv = None
            for gi in range(G):
                g = g0 + gi
                if S.is_ctx:
                    offs = []
                elif isA:
                    offs = offs_of(g, True)
                    cls = 1 if g == 4 else (2 if g == 35 else 0)
                else:
                    offs = offs_of(g, False)
                    if g in EDGE_TOP:
                        cls = 1 + EDGE_TOP.index(g)
                    elif g in EDGE_BOT:
                        cls = 3 + EDGE_BOT.index(g)
                    else:
                        cls = 0
                    bb, bbb = bB_pool.items[bB_pool.i % 2]
                    bsem = bB_sem[bB_pool.i % 2]
                    bB_pool.i += 1
                    nof = len(offs)
                    k.dma(k.pool, bb[:, :nof * 128].rearrange("p (o q) -> p o q", q=128),
                          biasB_d[L, cls, hh, :, offs[0] + 3:offs[0] + 3 + nof, :], bsem, (), [bbb])
                n_loc = len(offs)
                n = n_loc + 2
                pp = (att_ctr[0] % 2) * 2
                att_ctr[0] += 1
                qg = q[:, gi * 128:(gi + 1) * 128]

                def S_ap(j):
                    return PS(pp + j // 4)[:, (j % 4) * 128:(j % 4 + 1) * 128]

                def S_buf(j):
                    return psb[pp + j // 4]

                for bank in range(2):
                    js = [j for j in range(n) if j // 4 == bank]
                    if not js:
                        continue
                    terms = []
                    for j in js:
                        if j < n_loc:
                            c = g + offs[j] - c_lo
                            bias_ap = bAS[:, cls, j, :] if isA else bb[:, j * 128:(j + 1) * 128]
                            bias_buf = prmb if isA else bbb
                            terms.append((S_ap(j), kt[:, c * 128:(c + 1) * 128], qg, True, False, [ktb, qb]))
                            terms.append((S_ap(j), ident, bias_ap, False, True, [bias_buf, prm2b]))
                        else:
                            cc = j - n_loc
                            terms.append((S_ap(j), ckT[:, kvslot, cc * 128:(cc + 1) * 128], qg, True, True, [ckTb, qb]))
                    k.mmgroup(psb[pp + bank], terms)
                pT, pTb = pT_pool.get()
                for bank in range(2):
                    js = [j for j in range(n) if j // 4 == bank]
                    if not js:
                        continue
                    m = len(js)
                    k.act(pT[:, bank * 512:bank * 512 + m * 128], PS(pp + bank)[:, 0:m * 128], AF.Exp,
                          [psb[pp + bank]], [pTb])
                to, tdn = [], []
                for j in range(n):
                    pj = pT[:, j * 128:(j + 1) * 128]
                    if j < n_loc:
                        c = g + offs[j] - c_lo
                        if isA:
                            lv = vA_all[:, c, (hh // 3) * 128:(hh // 3 + 1) * 128]
                            lvb = vAb
                        else:
                            lv = vB_all[:, c, hh * 128:(hh + 1) * 128]
                            lvb = vBb
                    else:
                        lv = cv[:, j - n_loc, kvslot, :]
                        lvb = cvb
                    to.append((PS(po)[:, gi * 128:(gi + 1) * 128], lv, pj, j == 0, j == n - 1, [lvb, pTb]))
                    tdn.append((PS(pdn)[:, gi * 128:(gi + 1) * 128], ones, pj, j == 0, j == n - 1, [pTb, prm2b]))
                if pend_pv is not None:
                    k.mmgroup(psb[po], pend_pv[0])
                    k.mmgroup(psb[pdn], pend_pv[1])
                pend_pv = (to, tdn)
            k.mmgroup(psb[po], pend_pv[0])
            k.mmgroup(psb[pdn], pend_pv[1])
            r, rb = rs_pool.get()
            dn_, dnb_ = tf_pool.get()
            if isA:
                k.act(dn_[:, :T], PS(pdn)[:, :T], AF.Ln, [psb[pdn], esinkb], [dnb_], bias=esink[:, L, hh:hh + 1])
            else:
                k.act(dn_[:, :T], PS(pdn)[:, :T], AF.Ln, [psb[pdn]], [dnb_])
            k.act(r[:, :T], dn_[:, :T], AF.Exp, [dnb_], [rb], scale=-1.0)
            k.tt(xn[:, hd, :T], PS(po)[:, :T], r[:, :T], ALU.mult, [psb[po], rb], [xnb[hd]])
            if after_head is not None:
                after_head(hd)

    def conv_prep(S, L, t0, T):
        HH = cHH_s if S.is_ctx else HH_s[L]
        hname = "cHH" if S.is_ctx else "HH%d" % L
        k.dma(k.sp, hhc[:, :, :T + 30], HH[:, :, PADC + t0 - 15:PADC + t0 + T + 15].rearrange("c p t -> p c t"),
              hhc_sem, [sbuf_of(hname), sbuf_of("pads")], [hhcb])

    def conv_taps(L, T, lo, hi):
        for idx in range(lo, min(hi, 124)):
            cc, j = idx // 31, idx % 31
            if j == 0:
                k.ts(acc[cc][:, :T], hhc[:, cc, 0:T], dwS[:, L, cc, 0:1], cpS[:, L, 0, cc:cc + 1], ALU.mult, ALU.add,
                     [hhcb, prmb], [accb[cc]])
            else:
                k.stt(acc[cc][:, :T], hhc[:, cc, j:j + T], dwS[:, L, cc, j:j + 1], acc[cc][:, :T], ALU.mult, ALU.add,
                      [hhcb, prmb, accb[cc]], [accb[cc]])
            if j == 30:
                k.act(hT[:, cc, :T], acc[cc][:, :T], AF.Copy, [accb[cc]], [hTb[cc]])
                k.act(hT[:, 4 + cc, :T], acc[cc][:, :T], AF.Square, [accb[cc]], [hTb[4 + cc]])

    def conv_finish(S, L, t0, T):
        k.mmgroup(psb[6], [(PS(6)[:, :T], ones, hT[:, cc, :T], cc == 0, cc == 3, [hTb[cc], prm2b]) for cc in range(4)])
        k.mmgroup(psb[7], [(PS(7)[:, :T], ones, hT[:, 4 + cc, :T], cc == 0, cc == 3, [hTb[4 + cc], prm2b]) for cc in range(4)])
        m, mb = rs_pool.get()
        k.ts(m[:, :T], PS(6)[:, :T], 1.0 / 512.0, None, ALU.mult, None, [psb[6]], [mb])
        msq, msqb = tf_pool.get()
        k.tt(msq[:, :T], m[:, :T], m[:, :T], ALU.mult, [mb], [msqb])
        var, varb = tf_pool.get()
        k.stt(var[:, :T], PS(7)[:, :T], 1.0 / 512.0, msq[:, :T], ALU.mult, ALU.subtract, [psb[7], msqb], [varb])
        sd_, sdb_ = tf_pool.get()
        k.act(sd_[:, :T], var[:, :T], AF.Ln, [varb], [sdb_], bias=EPS, scale=1.0)
        r, rb = rs_pool.get()
        k.act(r[:, :T], sd_[:, :T], AF.Exp, [sdb_], [rb], scale=-0.5)
        for cc in range(4):
            z, zb = tf_pool.get()
            k.tt(z[:, :T], acc[cc][:, :T], m[:, :T], ALU.subtract, [accb[cc], mb], [zb])
            z2, z2b = tf_pool.get()
            k.tt(z2[:, :T], z[:, :T], r[:, :T], ALU.mult, [zb, rb], [z2b])
            k.act(xn[:, 12 + cc, :T], z2[:, :T], AF.Silu, [z2b, prmb], [xnb[12 + cc]],
                  bias=cpS[:, L, 2, cc:cc + 1], scale=cpS[:, L, 1, cc:cc + 1])

    wo_ctr = [0]

    def wout_step(S, L, T, sq_hook=False):
        coef = der[L][:, S.col, 5, :]
        for g in range(4):
            w, wb = k.wnext(("wout", L, g))
            wv = _v3(w[:, 0:8192], 512)
            for s in range(4):
                o = g * 4 + s
                pd = wo_ctr[0] % 4
                wo_ctr[0] += 1
                terms = [(PS(pd)[:, :T], wv[:, kk, s * 128:(s + 1) * 128], xn[:, kk, :T], kk == 0, kk == 15,
                          [wb, xnb[kk]]) for kk in range(16)]
                k.mmgroup(psb[pd], terms)
                k.stt(xT[:, o, :T], PS(pd)[:, :T], coef[:, o:o + 1], xT[:, o, :T], ALU.mult, ALU.add,
                      [psb[pd], xTb[o], derb[L]], [xTb[o]])
                if sq_hook:
                    k.act(hT[:, o, :T], xT[:, o, :T], AF.Square, [xTb[o]], [hTb[o]])

    def load_x(src2d, T, rname=None):
        rd = [sbuf_of(rname)] if rname else []
        for h in range(2):
            k.dma(k.sp, xT[:, 8 * h:8 * h + 8, :T],
                  src2d[1024 * h:1024 * (h + 1), :].rearrange("(k p) t -> p k t", p=128), x_sem, rd, xTb[8 * h:8 * h + 8])

    def store_x(dst2d, T, sem, wname=None):
        wr = [sbuf_of(wname)] if wname else []
        for h in range(2):
            k.dma(k.sp, dst2d[1024 * h:1024 * (h + 1), :].rearrange("(k p) t -> p k t", p=128),
                  xT[:, 8 * h:8 * h + 8, :T], sem, xTb[8 * h:8 * h + 8], wr)

    mod_step(0)
    for i in range(NT1 if debug != 1 else 2):
        t0 = i * TT
        load_x(xT_d[:, t0:t0 + TT], TT)
        norm_step(lat, 0, TT, 0)
        ffn_step(lat, 0, 0, TT, sq_hook=True, mod_inter=True)
        store_x(x1_s[0][:, t0:t0 + TT], TT, st_sem, "x1_0")
        norm_step(lat, 0, TT, 1, presq="xn")
        win_step(lat, 0, t0, TT, ALLG)
        if i == 0:
            load_x(ctxT_d, CTX)
            norm_step(ctxs, 0, CTX, 0)
            ffn_step(ctxs, 0, 0, CTX, sq_hook=True)
            store_x(cx1_s, CTX, st_sem, "cx1")
            norm_step(ctxs, 0, CTX, 1, presq="xn")
            win_step(ctxs, 0, 0, CTX, ALLG)
    if debug:
        dbg_d = nc.dram_tensor("dbg_der", [128, 2, 2 * 9 * 16], F32, kind="ExternalOutput").ap()
        for l in range(2):
            k.dma(k.sp, dbg_d[:, l, :], der[l][:].rearrange("p a b c -> p (a b c)"), out_sem, [derb[l]], [])
        dbg_m = nc.dram_tensor("dbg_mod", [128, 2, 288], F32, kind="ExternalOutput").ap()
        for l in range(2):
            k.dma(k.sp, dbg_m[:, l, :], modT[l][:].rearrange("p a b -> p (a b)"), out_sem, [modb[l]], [])
        dbg_c = nc.dram_tensor("dbg_ck", [128, 8 * CTX], BF16, kind="ExternalOutput").ap()
        k.dma(k.sp, dbg_c, ckT[:].rearrange("p a b -> p (a b)"), out_sem, [ckTb], [])
        dbg_v = nc.dram_tensor("dbg_cv", [128, 2 * 8 * 128], BF16, kind="ExternalOutput").ap()
        k.dma(k.sp, dbg_v, cv[:].rearrange("p a b c -> p (a b c)"), out_sem, [cvb], [])
    def ctx_phase2():
        load_x(cx1_s, CTX, "cx1")
        attn_step(ctxs, 0, 0, CTX, after_head=lambda hd: conv_taps(0, CTX, hd * 11, (hd + 1) * 11))
        conv_finish(ctxs, 0, 0, CTX)
        wout_step(ctxs, 0, CTX, sq_hook=True)
        norm_step(ctxs, 0, CTX, 2, presq="hT")
        ffn_step(ctxs, 0, 1, CTX, sq_hook=True)
        norm_step(ctxs, 1, CTX, 0, presq="xn")
        ffn_step(ctxs, 1, 0, CTX)
        store_x(cx1_s, CTX, st_sem, "cx1")

    def rest_of_program():
        for i in range(NT2):
            t0 = 256 + i * TT
            load_x(x1_s[0][:, t0:t0 + TT], TT, "x1_0")
            attn_step(lat, 0, t0, TT, after_head=lambda hd: conv_taps(0, TT, hd * 11, (hd + 1) * 11))
            conv_finish(lat, 0, t0, TT)
            wout_step(lat, 0, TT, sq_hook=True)
            if 0 < i < NT2 - 1:
                attn_prefetch(lat, 0, t0 + TT, TT)
            norm_step(lat, 0, TT, 2, presq="hT")
            ffn_step(lat, 0, 1, TT, sq_hook=True)
            norm_step(lat, 1, TT, 0, presq="xn")
            ffn_step(lat, 1, 0, TT, sq_hook=True)
            store_x(x1_s[1][:, t0:t0 + TT], TT, st_sem, "x1_1")
            norm_step(lat, 1, TT, 1, presq="xn")
            win_step(lat, 1, t0, TT, ALLG)
            if i == 0:
                ctx_phase2()
        load_x(cx1_s, CTX, "cx1")
        norm_step(ctxs, 1, CTX, 1)
        win_step(ctxs, 1, 0, CTX, KVG)
        for i in range(NT3):
            t0 = HALO + i * TT
            load_x(x1_s[1][:, t0:t0 + TT], TT, "x1_1")
            attn_step(lat, 1, t0, TT, after_head=lambda hd: conv_taps(1, TT, hd * 11, (hd + 1) * 11))
            conv_finish(lat, 1, t0, TT)
            wout_step(lat, 1, TT, sq_hook=True)
            if i < NT3 - 1:
                attn_prefetch(lat, 1, t0 + TT, TT)
            norm_step(lat, 1, TT, 2, presq="hT")
            ffn_step(lat, 1, 1, TT)
            store_x(out_d[:, t0 - HALO:t0 - HALO + TT], TT, out_sem)

    if debug != 1:
        rest_of_program()
    for i_ in range(3):
        k.pool.prog.append(("w", k.ws_sem[i_].h, k.ws_sem[i_].total))
    k.pool.prog.append(("w", k.wst_sem.h, k.wst_sem.total))
    k.sp.prog.append(("w", out_sem.h, out_sem.total))
    k.sp.prog.append(("w", st_sem.h, st_sem.total))

    def emit(eng_obj, prog):
        for it in prog:
            if it[0] == "w":
                eng_obj.wait_ge(it[1], it[2])
            else:
                ins = it[1](eng_obj)
                if it[2] is not None:
                    ins.then_inc(it[2], it[3])

    with nc.Block() as block:
        @block.tensor
        def _(e):
            emit(e, k.pe.prog)

        @block.scalar
        def _(e):
            emit(e, k.act_e.prog)

        @block.vector
        def _(e):
            emit(e, k.dve.prog)

        @block.gpsimd
        def _(e):
            emit(e, k.pool.prog)

        @block.sync
        def _(e):
            emit(e, k.sp.prog)
    k.stats = {n: len(e.prog) for n, e in (("pe", k.pe), ("act", k.act_e), ("dve", k.dve), ("pool", k.pool), ("sp", k.sp))}
    print("program sizes", k.stats, "nsem", k.nsem, "sbuf_free", nc.sbuf_bytes_remaining)


def _rope_tables(t_ext):
    row = (np.clip(t_ext, 0, SEQ - 1) // 64).astype(np.float32)
    col = (np.clip(t_ext, 0, SEQ - 1) % 64).astype(np.float32)
    n_freq = 32
    inv_freq = (np.float32(10000.0) ** (-np.arange(n_freq, dtype=np.float32) / np.float32(n_freq))).astype(np.float32)
    p = np.arange(128)
    f = p % 32
    half = (p % 64) // 32
    pos = np.where((p < 64)[:, None], row[None, :], col[None, :]).astype(np.float32)
    ang = (pos * inv_freq[f][:, None]).astype(np.float32)
    cos = np.cos(ang).astype(np.float32)
    sin = np.sin(ang).astype(np.float32) * np.where(half == 0, -1.0, 1.0).astype(np.float32)[:, None]
    return cos, sin.astype(np.float32)


def _consts():
    c = np.zeros((128, 3, 128), np.float32)
    c[:, 0, :] = 1.0
    c[:, 1, :] = np.eye(128, dtype=np.float32)
    p = np.arange(128)
    half = (p % 64) // 32
    pm = np.where(half == 0, p + 32, p - 32)
    c[pm, 2, p] = 1.0
    return c


def _biasA(j):
    kk = np.arange(128)[:, None]
    qq = np.arange(128)[None, :]
    t = np.zeros((128, 3, 3, 128), np.float32)
    for cls in range(3):
        t[:, cls, 0, :] = np.where(kk >= qq, 0.0, NEG)
        t[:, cls, 1, :] = 0.0
        t[:, cls, 2, :] = np.where(kk <= qq, 0.0, NEG)
    if j == 0:
        t[:, 1, 0, :] = NEG
    if j == 3:
        t[:, 2, 2, :] = NEG
    return t


def _biasB(rpb, j):
    out = np.full((2, 5, 6, 128, 7, 128), NEG, np.float32)
    kr = (np.arange(128) // 64)[:, None, None]
    kc = (np.arange(128) % 64)[:, None, None]
    oo = (np.arange(7) - 3)[None, :, None]
    qr = (np.arange(128) // 64)[None, None, :]
    qc = (np.arange(128) % 64)[None, None, :]
    drow = 2 * oo + kr - qr
    dcol = kc - qc
    cs = np.clip(qc - 8, 0, 48)
    col_ok = (kc >= cs) & (kc <= cs + 15)
    for cls in range(5):
        lo = -4 * np.ones_like(qr)
        if cls in (1, 2) and j == 0:
            r = (cls - 1) * 2 + qr
            lo = -r
        elif cls in (3, 4) and j == 3:
            r = 252 + (cls - 3) * 2 + qr
            lo = 248 - r
        ok = (drow >= lo) & (drow <= lo + 7) & col_ok
        di = np.clip(drow + 7, 0, 14)
        dj = np.clip(dcol + 15, 0, 30)
        di_b, dj_b = np.broadcast_arrays(di, dj)
        for l in range(2):
            for h in range(6):
                vals = rpb[l, h][di_b, dj_b]
                out[l, cls, h] = np.where(ok, vals, np.float32(NEG))
    return out


def prep_core_inputs(c, inp):
    b, j = c // 4, c % 4
    s0 = j * OWN - HALO
    t_ext = s0 + np.arange(NEXT)
    valid = (t_ext >= 0) & (t_ext < SEQ)
    xT = np.zeros((D, NEXT), np.float32)
    lo, hi = max(s0, 0), min(s0 + NEXT, SEQ)
    xT[:, lo - s0:hi - s0] = inp["x"][b, lo:hi, :].T
    m = {}
    m["xT"] = xT
    m["ctxT"] = np.ascontiguousarray(inp["ctx"][b].T)
    cm = np.zeros((128, 16, 2), np.float32)
    cm[:, :, 0] = inp["c"][b].reshape(16, 128).T
    cm[:, :, 1] = inp["c_ctx"].reshape(16, 128).T
    m["cm"] = cm
    m["w_mod"] = inp["w_mod"]
    m["b_modT"] = np.ascontiguousarray(inp["b_mod"].reshape(2, 144, 128).transpose(0, 2, 1))
    gn = np.stack([inp["norm_ffn1"], inp["norm_mix"], inp["norm_ffn2"]], axis=1)
    m["gnT"] = np.ascontiguousarray(gn.reshape(2, 3, 16, 128).transpose(3, 0, 1, 2))
    for n in ("ffn1_w_gate", "ffn1_w_up", "ffn1_w_down", "ffn2_w_gate", "ffn2_w_up", "ffn2_w_down", "w_in", "w_out"):
        m[n] = inp[n]
    qkg = np.stack([inp["a_q_norm"], inp["a_k_norm"], inp["b_q_norm"], inp["b_k_norm"]], axis=-1)
    m["qkg"] = np.ascontiguousarray(qkg.transpose(1, 0, 2))
    m["sinkT"] = np.ascontiguousarray(np.broadcast_to(inp["a_sink"][None], (128, 2, 6)))
    m["biasA"] = _biasA(j)
    m["biasB"] = _biasB(inp["b_rpb"], j)
    cos, sin = _rope_tables(t_ext)
    m["cosT"] = cos
    m["sinT"] = sin
    m["validT"] = np.ascontiguousarray(np.broadcast_to(valid.astype(np.float32)[None], (128, NEXT)))
    m["dwT"] = np.ascontiguousarray(inp["c_dw_w"].reshape(2, 31, 4, 128).transpose(3, 0, 2, 1))
    cp = np.stack([inp["c_dw_b"], inp["c_ln_g"], inp["c_ln_b"]], axis=1)
    m["cpT"] = np.ascontiguousarray(cp.reshape(2, 3, 4, 128).transpose(3, 0, 1, 2))
    m["consts"] = _consts()
    return m


_NC_CACHE = {}


def kernel(**inputs):
    inp = {k_: np.asarray(v) for k_, v in inputs.items()}
    if "nc" not in _NC_CACHE:
        _NC_CACHE["nc"] = build_nc()
    nc = _NC_CACHE["nc"]
    in_maps = [prep_core_inputs(c, inp) for c in range(NCORE)]
    res = run_bass_kernel_spmd(nc, in_maps, core_ids=list(range(NCORE)))
    out = np.empty((2, SEQ, D), np.float32)
    for c in range(NCORE):
        b, j = c // 4, c % 4
        out[b, j * OWN:(j + 1) * OWN, :] = res.results[c]["outT"].T
    return out
```

```python
import contextlib
import math
import numpy as np
import concourse.bass as bass
import concourse.mybir as mybir
from concourse.bass_utils import run_bass_kernel_spmd

F32 = mybir.dt.float32
BF16 = mybir.dt.bfloat16
AF = mybir.ActivationFunctionType
ALU = mybir.AluOpType

D = 2048
DFF = 5632
NCORE = 8
SEQ = 16384
OWN = 4096
HALO = 512
NEXT = OWN + 2 * HALO
NCH = NEXT // 128
CTX = 256
TT = 512
EPS = 1e-6
NEG = -30000.0
PADC = 16
IN_COLS = 4608
SEM_LIMIT = 30000
PRECONV = True
PRECONV_EVERY = 4


class SemObj:
    def __init__(self, h, dma, owner=None):
        self.h = h
        self.total = 0
        self.dma = dma
        self.owner = owner


class Buf:
    __slots__ = ("name", "w", "r")

    def __init__(self, name):
        self.name = name
        self.w = None
        self.r = {}


class Eng:
    def __init__(self, name, is_pe=False):
        self.name = name
        self.prog = []
        self.seen = {}
        self.sem = None
        self.is_pe = is_pe


class Pool:
    def __init__(self, items):
        self.items = items
        self.i = 0

    def get(self):
        it = self.items[self.i % len(self.items)]
        self.i += 1
        return it


class Stream:
    pass


class K:
    def __init__(self, nc, stack):
        self.nc = nc
        self.stack = stack
        self.pe = Eng("pe", True)
        self.act_e = Eng("act")
        self.dve = Eng("dve")
        self.pool = Eng("pool")
        self.sp = Eng("sp")
        self.nsem = 0

    def new_sem(self, dma, owner=None):
        self.nsem += 1
        h = self.stack.enter_context(self.nc.semaphore("s%d" % self.nsem))
        return SemObj(h, dma, owner)

    def sb(self, name, shape, dt):
        return self.stack.enter_context(self.nc.sbuf_tensor("sb_" + name, shape, dt))

    def _wait(self, eng, tickets):
        need = {}
        for t in tickets:
            if t is None:
                continue
            s, v = t
            if s.dma:
                v = s.total
            if eng.is_pe and s.owner is eng:
                continue
            if need.get(s, 0) < v:
                need[s] = v
        for s, v in need.items():
            if eng.seen.get(s, 0) < v:
                eng.prog.append(("w", s.h, v))
                eng.seen[s] = v

    def op(self, eng, fn, reads=(), writes=(), inc=True):
        tk = []
        for b in reads:
            tk.append(b.w)
        for b in writes:
            tk.append(b.w)
            tk.extend(b.r.items())
        self._wait(eng, tk)
        if inc:
            if eng.sem is None or eng.sem.total >= SEM_LIMIT:
                eng.sem = self.new_sem(False, eng)
            s = eng.sem
            s.total += 1
            eng.prog.append(("i", fn, s.h, 1))
            t = (s, s.total)
            for b in writes:
                b.w = t
                b.r = {}
            for b in reads:
                if b.r.get(s, 0) < s.total:
                    b.r[s] = s.total
        else:
            eng.prog.append(("i", fn, None, 0))

    def dma(self, q, out_ap, in_ap, sem, reads=(), writes=()):
        tk = [b.w for b in reads]
        for b in writes:
            if not (b.w is not None and b.w[0] is sem):
                tk.append(b.w)
            tk.extend(b.r.items())
        self._wait(q, tk)
        sem.total += 16
        q.prog.append(("i", (lambda e: e.dma_start(out=out_ap, in_=in_ap)), sem.h, 16))
        t = (sem, sem.total)
        for b in writes:
            b.w = t
            b.r = {}
        for b in reads:
            b.r[sem] = sem.total

    def mmgroup(self, outbuf, terms):
        allr = []
        for t in terms:
            for b in t[5]:
                if b not in allr:
                    allr.append(b)
        n = len(terms)
        for i, (o, l, r, st, sp_, bs) in enumerate(terms):
            last = i == n - 1
            fn = (lambda e, o=o, l=l, r=r, st=st, sp_=sp_: e.matmul(o, l, r, start=st, stop=sp_))
            self.op(self.pe, fn, reads=(allr if last else bs), writes=[outbuf], inc=last)

    def act(self, out, in_, func, reads, writes, bias=None, scale=None):
        kw = {}
        if bias is not None:
            kw["bias"] = bias
        if scale is not None:
            kw["scale"] = scale
        self.op(self.act_e, (lambda e: e.activation(out=out, in_=in_, func=func, **kw)), reads, writes)

    def tt(self, out, in0, in1, op, reads, writes, eng=None):
        self.op(eng or self.dve, (lambda e: e.tensor_tensor(out=out, in0=in0, in1=in1, op=op)), reads, writes)

    def ts(self, out, in0, s1, s2, op0, op1, reads, writes, eng=None):
        if op1 is None:
            fn = (lambda e: e.tensor_scalar(out=out, in0=in0, scalar1=s1, scalar2=None, op0=op0))
        else:
            fn = (lambda e: e.tensor_scalar(out=out, in0=in0, scalar1=s1, scalar2=s2, op0=op0, op1=op1))
        self.op(eng or self.dve, fn, reads, writes)

    def stt(self, out, in0, scalar, in1, op0, op1, reads, writes, eng=None):
        self.op(eng or self.dve,
                (lambda e: e.scalar_tensor_tensor(out=out, in0=in0, scalar=scalar, in1=in1, op0=op0, op1=op1)),
                reads, writes)

    def recip(self, out, in_, reads, writes):
        self.op(self.dve, (lambda e: e.reciprocal(out=out, in_=in_)), reads, writes)

    def tcopy(self, out, in_, reads, writes, eng=None):
        self.op(eng or self.dve, (lambda e: e.tensor_copy(out=out, in_=in_)), reads, writes)

    def memset(self, ap, val, writes, eng=None):
        self.op(eng or self.dve, (lambda e: e.memset(ap, val)), (), writes)

    def wplan_set(self, gen):
        self.wgen = gen
        self.wissued = []
        self.wnext_i = 0
        self.wgen_done = False
        self.wcache = {}
        self.wcache_n = 0
        self.wpending = None
        self.wst_sem = self.new_sem(True)
        self.wevents = 0
        self.preconv = []
        self.preconv_on = False

    def _wissue_one(self):
        try:
            key, dmas, nel = next(self.wgen)
        except StopIteration:
            self.wgen_done = True
            if self.wpending is not None:
                self.wpending()
                self.wpending = None
            return
        i = len(self.wissued)
        s = i % 3
        cached = self.wcache.get(key) if nel else None
        if cached is not None:
            cap, cbuf = cached
            self.dma(self.pool, self.ws[s][:, 0:nel], cap, self.ws_sem[s], reads=[cbuf], writes=[self.wsb[s]])
        else:
            for (dst_fn, src) in dmas:
                self.dma(self.pool, dst_fn(self.ws[s]), src, self.ws_sem[s], reads=(), writes=[self.wsb[s]])
        if self.wpending is not None:
            self.wpending()
            self.wpending = None
        self.wevents += 1
        if self.preconv_on and self.preconv and self.wevents % PRECONV_EVERY == 0:
            pkey, pdmas, pnel = self.preconv.pop(0)
            if pkey not in self.wcache:
                self.wcache_n += 1
                pcap = self.nc.dram_tensor("wc%d" % self.wcache_n, [128, pnel], BF16, kind="Internal").ap()
                pcbuf = Buf("wc%d" % self.wcache_n)
                for (dst_fn, src) in pdmas:
                    self.dma(self.pool, dst_fn(pcap), src, self.wst_sem, reads=(), writes=[pcbuf])
                self.wcache[pkey] = (pcap, pcbuf)
        if nel and cached is None:
            self.wcache_n += 1
            cap = self.nc.dram_tensor("wc%d" % self.wcache_n, [128, nel], BF16, kind="Internal").ap()
            cbuf = Buf("wc%d" % self.wcache_n)
            self.wcache[key] = (cap, cbuf)

            def _store(s=s, cap=cap, cbuf=cbuf, nel=nel):
                self.dma(self.pool, cap, self.ws[s][:, 0:nel], self.wst_sem, reads=[self.wsb[s]], writes=[cbuf])
            self.wpending = _store
        self.wissued.append(key)

    def wnext(self, key):
        i = self.wnext_i
        while len(self.wissued) < i + 3 and not self.wgen_done:
            self._wissue_one()
        assert self.wissued[i] == key, (self.wissued[i], key)
        self.wnext_i += 1
        s = i % 3
        return self.ws[s], self.wsb[s]


def _v3(ap2d, c):
    return ap2d.rearrange("p (k c) -> p k c", c=c)


def build_nc(debug=False):
    nc = bass.Bass("TRN2", target_bir_lowering=False)
    stack = contextlib.ExitStack()
    with stack:
        _build(nc, stack, debug)
    return nc


def _build(nc, stack, debug):
    k = K(nc, stack)

    def din(name, shape, dt=F32):
        return nc.dram_tensor(name, list(shape), dt, kind="ExternalInput").ap()

    def dscr(name, shape, dt):
        return nc.dram_tensor(name, list(shape), dt, kind=("ExternalOutput" if debug else "Internal")).ap()

    xT_d = din("xT", [D, NEXT])
    ctxT_d = din("ctxT", [D, CTX])
    cm_d = din("cm", [128, 16, 2])
    wmod_d = din("w_mod", [2, D, 9 * D])
    bmod_d = din("b_modT", [2, 128, 144])
    gn_d = din("gnT", [128, 2, 3, 16])
    wg_d = [din("ffn1_w_gate", [2, D, DFF]), din("ffn2_w_gate", [2, D, DFF])]
    wu_d = [din("ffn1_w_up", [2, D, DFF]), din("ffn2_w_up", [2, D, DFF])]
    wd_d = [din("ffn1_w_down", [2, DFF, D]), din("ffn2_w_down", [2, DFF, D])]
    win_d = din("w_in", [2, D, IN_COLS])
    wout_d = din("w_out", [2, D, D])
    qkg_d = din("qkg", [128, 2, 4])
    sink_d = din("sinkT", [128, 2, 6])
    biasA_d = din("biasA", [128, 3, 3, 128])
    biasB_d = din("biasB", [2, 5, 6, 128, 7, 128])
    cos_d = din("cosT", [128, NEXT])
    sin_d = din("sinT", [128, NEXT])
    valid_d = din("validT", [128, NEXT])
    dw_d = din("dwT", [128, 2, 4, 31])
    cp_d = din("cpT", [128, 2, 3, 4])
    consts_d = din("consts", [128, 3, 128])
    out_d = nc.dram_tensor("outT", [D, OWN], F32, kind="ExternalOutput").ap()

    x1_s = [dscr("x1s%d" % l, [D, NEXT], F32) for l in range(2)]
    QT_s = [dscr("QTs%d" % l, [12, 128, NEXT], BF16) for l in range(2)]
    KT_s = [dscr("KTs%d" % l, [8, 128, NEXT], BF16) for l in range(2)]
    VA_s = [dscr("VAs%d" % l, [NEXT, 256], BF16) for l in range(2)]
    VB_s = [dscr("VBs%d" % l, [NEXT, 768], BF16) for l in range(2)]
    HH_s = [dscr("HHs%d" % l, [4, 128, NEXT + 2 * PADC], BF16) for l in range(2)]
    cx1_s = dscr("cx1s", [D, CTX], F32)
    cQT_s = dscr("cQTs", [12, 128, CTX], BF16)
    cHH_s = dscr("cHHs", [4, 128, CTX + 2 * PADC], BF16)

    xT = k.sb("xT", [128, 16, TT], F32)
    xn = k.sb("xn", [128, 16, TT], BF16)
    hT = k.sb("hT", [128, 22, TT], BF16)
    k.ws = [k.sb("ws%d" % i, [128, 8192], BF16) for i in range(3)]
    k.wsb = [Buf("ws%d" % i) for i in range(3)]
    k.ws_sem = [k.new_sem(True) for _ in range(3)]
    xTb = [Buf("xT%d" % i) for i in range(16)]
    xnb = [Buf("xn%d" % i) for i in range(16)]
    hTb = [Buf("hT%d" % i) for i in range(22)]
    ps_t = stack.enter_context(nc.psum_tensor("ps", [128, 8, 512], F32))
    psb = [Buf("ps%d" % i) for i in range(8)]

    def mkpool(name, n, shape, dt):
        items = []
        for i in range(n):
            items.append((k.sb("%s%d" % (name, i), shape, dt), Buf("%s%d" % (name, i))))
        return Pool(items)

    rs_pool = mkpool("rs", 3, [128, TT], F32)
    tf_pool = mkpool("tf", 4, [128, TT], F32)
    tb_pool = mkpool("tb", 6, [128, TT], BF16)
    vst_pool = mkpool("vst", 2, [128, 512], BF16)
    qT_pool = mkpool("qTh", 2, [128, TT], BF16)
    kT_pool = mkpool("kTh", 2, [128, 10 * 128], BF16)
    vA_all = k.sb("vA_all", [128, 6, 256], BF16)
    vB_all = k.sb("vB_all", [128, 9, 768], BF16)
    vAb, vBb = Buf("vA_all"), Buf("vB_all")
    vA_sem, vB_sem = k.new_sem(True), k.new_sem(True)
    bB_pool = mkpool("bB", 2, [128, 6 * 128], BF16)
    pT_pool = mkpool("pT", 2, [128, 8 * 128], BF16)
    q_sems = [k.new_sem(True) for _ in range(2)]
    k_sems = [k.new_sem(True) for _ in range(2)]
    bB_sem = [k.new_sem(True), k.new_sem(True)]
    ckT = k.sb("ckT", [128, 8, CTX], BF16)
    cv = k.sb("cv", [128, 2, 8, 128], BF16)
    ckTb = Buf("ckT")
    cvb = Buf("cv")
    hhc = k.sb("hhc", [128, 4, TT + 30], BF16)
    hhcb = Buf("hhc")
    hhc_sem = k.new_sem(True)
    acc = [k.sb("acc%d" % i, [128, TT], F32) for i in range(4)]
    accb = [Buf("acc%d" % i) for i in range(4)]
    cst = k.sb("cst", [128, 3, 128], BF16)
    cstb = Buf("cst")
    cosS = k.sb("cosS", [128, TT], F32)
    sinS = k.sb("sinS", [128, TT], F32)
    valS = k.sb("valS", [128, TT], F32)
    cosb, sinb, valb = Buf("cos"), Buf("sin"), Buf("val")
    tab_sem = k.new_sem(True)
    cmS = k.sb("cmS", [128, 16, 2], F32)
    scS = k.sb("scS", [128, 16, 2], BF16)
    modT = [k.sb("modT%d" % l, [128, 144, 2], F32) for l in range(2)]
    der = [k.sb("der%d" % l, [128, 2, 9, 16], F32) for l in range(2)]
    bmodS = k.sb("bmodS", [128, 2, 144], F32)
    gnS = k.sb("gnS", [128, 2, 3, 16], F32)
    qkgS = k.sb("qkgS", [128, 2, 4], F32)
    sinkS = k.sb("sinkS", [128, 2, 6], F32)
    bAS = k.sb("bAS", [128, 3, 3, 128], BF16)
    dwS = k.sb("dwS", [128, 2, 4, 31], F32)
    cpS = k.sb("cpS", [128, 2, 3, 4], F32)
    zer = k.sb("zer", [128, PADC], BF16)
    prmb = Buf("params")
    modb = [Buf("mod0"), Buf("mod1")]
    derb = [Buf("der0"), Buf("der1")]
    prm_sem = k.new_sem(True)
    x_sem = k.new_sem(True)
    st_sem = k.new_sem(True)
    out_sem = k.new_sem(True)
    scrb = {}

    def sbuf_of(name):
        if name not in scrb:
            scrb[name] = Buf(name)
        return scrb[name]

    ones = cst[:, 0, :]
    ident = cst[:, 1, :]
    perm = cst[:, 2, :]

    def PS(i):
        return ps_t[:, i, :]

    lat = Stream()
    lat.is_ctx = False
    lat.col = 0
    ctxs = Stream()
    ctxs.is_ctx = True
    ctxs.col = 1

    def w_mod_loads(L):
        for g in range(36):
            src = wmod_d[L, :, g * 512:(g + 1) * 512].rearrange("(k p) c -> p k c", p=128)
            yield (("mod", L, g), [((lambda w: _v3(w[:, 0:8192], 512)), src)], 0)

    plan_mod_i = [0]

    def w_ffn_loads(L, which, mod_inter=False):
        wg, wu, wd = wg_d[which][L], wu_d[which][L], wd_d[which][L]
        for hf in range(2):
            for gi in range(11):
                if mod_inter and gi % 4 == 3 and plan_mod_i[0] < 36:
                    g_ = plan_mod_i[0]
                    plan_mod_i[0] += 1
                    src_ = wmod_d[1, :, g_ * 512:(g_ + 1) * 512].rearrange("(k p) c -> p k c", p=128)
                    yield (("mod", 1, g_), [((lambda w: _v3(w[:, 0:8192], 512)), src_)], 0)
                c0 = hf * 2816 + gi * 256
                sg = wg[:, c0:c0 + 256].rearrange("(k p) c -> p k c", p=128)
                su = wu[:, c0:c0 + 256].rearrange("(k p) c -> p k c", p=128)
                yield (("gu", L, which, hf, gi), [((lambda w: _v3(w[:, 0:4096], 256)), sg),
                                                  ((lambda w: _v3(w[:, 4096:8192], 256)), su)], 8192)
            for oi in range(8):
                sd = wd[hf * 2816:(hf + 1) * 2816, oi * 256:(oi + 1) * 256].rearrange("(k p) c -> p k c", p=128)
                yield (("dn", L, which, hf, oi), [((lambda w: _v3(w[:, 0:5632], 256)), sd)], 5632)

    WIN_GROUPS = [[0, 1, 2, 3], [4, 5, 6, 7], [8, 9, 10, 11], [12, 13, 14, 15], [16, 17, 18, 19],
                  [20, 21, 22, 23], [24, 25, 26, 27], [28, 29, 32, 33], [30, 31, 34, 35]]

    def w_in_loads(L, groups):
        for gi in groups:
            blks = WIN_GROUPS[gi]
            if gi < 7:
                c0 = blks[0] * 128
                src = win_d[L, :, c0:c0 + 512].rearrange("(k p) c -> p k c", p=128)
                yield (("win", L, gi), [((lambda w: _v3(w[:, 0:8192], 512)), src)], 8192)
            else:
                ca = blks[0] * 128
                cg = blks[2] * 128
                s1 = win_d[L, :, ca:ca + 256].rearrange("(k p) c -> p k c", p=128)
                s2 = win_d[L, :, cg:cg + 256].rearrange("(k p) c -> p k c", p=128)
                yield (("win", L, gi), [((lambda w: _v3(w[:, 0:8192], 512)[:, :, 0:256]), s1),
                                        ((lambda w: _v3(w[:, 0:8192], 512)[:, :, 256:512]), s2)], 8192)

    def w_out_loads(L):
        for g in range(4):
            src = wout_d[L, :, g * 512:(g + 1) * 512].rearrange("(k p) c -> p k c", p=128)
            yield (("wout", L, g), [((lambda w: _v3(w[:, 0:8192], 512)), src)], 8192)

    ALLG = list(range(9))
    KVG = [1, 2, 4, 5, 6]
    NT1 = NEXT // TT
    NT2 = (OWN + 512) // TT
    NT3 = OWN // TT

    def plan():
        yield from w_mod_loads(0)
        for i_ in range(NT1):
            yield from w_ffn_loads(0, 0, mod_inter=True)
            yield from w_in_loads(0, ALLG)
            if i_ == 0:
                yield from w_ffn_loads(0, 0)
                yield from w_in_loads(0, ALLG)
        for i_ in range(NT2):
            yield from w_out_loads(0)
            yield from w_ffn_loads(0, 1)
            yield from w_ffn_loads(1, 0)
            yield from w_in_loads(1, ALLG)
            if i_ == 0:
                yield from w_out_loads(0)
                yield from w_ffn_loads(0, 1)
                yield from w_ffn_loads(1, 0)
        yield from w_in_loads(1, KVG)
        for _ in range(NT3):
            yield from w_out_loads(1)
            yield from w_ffn_loads(1, 1)

    k.wplan_set(plan())
    if PRECONV:
        for gen_ in (w_out_loads(0), w_ffn_loads(0, 1), w_ffn_loads(1, 0), w_in_loads(1, ALLG),
                     w_out_loads(1), w_ffn_loads(1, 1)):
            k.preconv.extend(list(gen_))

    def pload(dst, src, q=None):
        k.dma(q or k.sp, dst, src, prm_sem, reads=(), writes=[prmb])

    pload(cst[:], consts_d, q=k.pool)
    pload(bAS[:], biasA_d, q=k.pool)
    pload(cmS[:], cm_d)
    pload(bmodS[:, 0, :], bmod_d[0])
    pload(bmodS[:, 1, :], bmod_d[1])
    pload(gnS[:], gn_d)
    pload(qkgS[:], qkg_d)
    pload(sinkS[:], sink_d)
    pload(dwS[:], dw_d)
    pload(cpS[:], cp_d)
    esink = k.sb("esink", [128, 2, 6], F32)
    qkgE = k.sb("qkgE", [128, 2, 4], F32)
    prm2b = Buf("params2")
    esinkb = Buf("esink")
    scSb = Buf("scS")
    zerb = Buf("zer")
    k.act(esink[:], sinkS[:], AF.Exp, [prmb], [esinkb])
    k.act(scS[:], cmS[:], AF.Silu, [prmb], [scSb])
    k.ts(qkgE[:], qkgS[:], 1.0, None, ALU.mult, None, [prmb], [prm2b])
    for l in range(2):
        for j in (0, 2):
            k.ts(qkgE[:, l, j:j + 1], qkgS[:, l, j:j + 1], 128.0 ** -0.5, None, ALU.mult, None, [prmb, prm2b], [prm2b])
    k.memset(zer[:], 0.0, [zerb])
    for hs, n in ((HH_s[0], NEXT), (HH_s[1], NEXT), (cHH_s, CTX)):
        for c in range(4):
            k.dma(k.sp, hs[c, :, 0:PADC], zer[:], st_sem, reads=[zerb], writes=[sbuf_of("pads")])
            k.dma(k.sp, hs[c, :, PADC + n:PADC + n + PADC], zer[:], st_sem, reads=[zerb], writes=[sbuf_of("pads")])

    def mod_slot(L, g):
        pb = psb[7]
        w, wb = k.wnext(("mod", L, g))
        wv = _v3(w[:, 0:8192], 512)
        for s in range(4):
            terms = []
            for kk in range(16):
                terms.append((PS(7)[:, 2 * s:2 * s + 2], wv[:, kk, s * 128:(s + 1) * 128], scS[:, kk, :],
                              kk == 0, kk == 15, [wb, scSb]))
            k.mmgroup(pb, terms)
        pv = PS(7)[:, 0:8].rearrange("p (j c) -> p j c", c=2)
        for col in range(2):
            k.tt(modT[L][:, 4 * g:4 * g + 4, col], pv[:, :, col], bmodS[:, L, 4 * g:4 * g + 4], ALU.add,
                 [pb, prmb], [modb[L]])

    def mod_step(L):
        for g in range(36):
            mod_slot(L, g)
        mod_derive(L)

    def mod_derive(L):
        for col in range(2):
            for i in range(9):
                src = modT[L][:, i * 16:(i + 1) * 16, col]
                dst = der[L][:, col, i, :]
                if i in (0, 3, 6):
                    k.tcopy(dst, src, [modb[L]], [derb[L]])
                elif i in (1, 4, 7):
                    k.stt(dst, src, 1.0, gnS[:, L, i // 3, :], ALU.add, ALU.mult, [modb[L], prmb], [derb[L]])
                elif i in (2, 8):
                    k.ts(dst, src, 0.5, None, ALU.mult, None, [modb[L]], [derb[L]])
                else:
                    k.tcopy(dst, src, [modb[L]], [derb[L]])

    def norm_step(S, L, T, which, presq=None):
        a_ap = der[L][:, S.col, which * 3 + 1, :]
        sh_ap = der[L][:, S.col, which * 3 + 0, :]
        if presq is None:
            for kk in range(16):
                k.act(hT[:, kk, :T], xT[:, kk, :T], AF.Square, [xTb[kk]], [hTb[kk]])
        if presq == "xn":
            terms = [(PS(6)[:, :T], ones, xn[:, kk, :T], kk == 0, kk == 15, [xnb[kk], prm2b]) for kk in range(16)]
        else:
            terms = [(PS(6)[:, :T], ones, hT[:, kk, :T], kk == 0, kk == 15, [hTb[kk], prm2b]) for kk in range(16)]
        k.mmgroup(psb[6], terms)
        sq_, sqb_ = tf_pool.get()
        k.act(sq_[:, :T], PS(6)[:, :T], AF.Ln, [psb[6]], [sqb_], bias=EPS, scale=1.0 / 2048.0)
        r, rb = rs_pool.get()
        k.act(r[:, :T], sq_[:, :T], AF.Exp, [sqb_], [rb], scale=-0.5)
        for kk in range(16):
            t, tb = tf_pool.get()
            k.tt(t[:, :T], xT[:, kk, :T], r[:, :T], ALU.mult, [xTb[kk], rb], [tb])
            k.act(xn[:, kk, :T], t[:, :T], AF.Identity, [tb, derb[L]], [xnb[kk]],
                  bias=sh_ap[:, kk:kk + 1], scale=a_ap[:, kk:kk + 1])

    gu_ctr = [0]
    dn_ctr = [0]

    prog_mod_i = [0]

    def ffn_step(S, L, which, T, sq_hook=False, mod_inter=False, tap_hook=None):
        coef = der[L][:, S.col, (0 if which == 0 else 2) * 3 + 2, :]
        for hf in range(2):
            for gi in range(11):
                if mod_inter and gi % 4 == 3 and prog_mod_i[0] < 36:
                    mod_slot(1, prog_mod_i[0])
                    prog_mod_i[0] += 1
                    if prog_mod_i[0] == 36:
                        mod_derive(1)
                w, wb = k.wnext(("gu", L, which, hf, gi))
                wgv = _v3(w[:, 0:4096], 256)
                wuv = _v3(w[:, 4096:8192], 256)
                for s in range(2):
                    j = gi * 2 + s
                    pg = (gu_ctr[0] % 2) * 2
                    gu_ctr[0] += 1
                    tg = [(PS(pg)[:, :T], wgv[:, kk, s * 128:(s + 1) * 128], xn[:, kk, :T], kk == 0, kk == 15,
                           [wb, xnb[kk]]) for kk in range(16)]
                    k.mmgroup(psb[pg], tg)
                    tu = [(PS(pg + 1)[:, :T], wuv[:, kk, s * 128:(s + 1) * 128], xn[:, kk, :T], kk == 0, kk == 15,
                           [wb, xnb[kk]]) for kk in range(16)]
                    k.mmgroup(psb[pg + 1], tu)
                    sg, sgb = tf_pool.get()
                    k.act(sg[:, :T], PS(pg)[:, :T], AF.Silu, [psb[pg]], [sgb])
                    k.tt(hT[:, j, :T], PS(pg + 1)[:, :T], sg[:, :T], ALU.mult, [psb[pg + 1], sgb], [hTb[j]])
                    if tap_hook is not None:
                        tap_hook()
            for oi in range(8):
                w, wb = k.wnext(("dn", L, which, hf, oi))
                wdv = _v3(w[:, 0:5632], 256)
                for s in range(2):
                    o = oi * 2 + s
                    pd = 4 + (dn_ctr[0] % 2)
                    dn_ctr[0] += 1
                    td = [(PS(pd)[:, :T], wdv[:, kk, s * 128:(s + 1) * 128], hT[:, kk, :T], kk == 0, kk == 21,
                           [wb, hTb[kk]]) for kk in range(22)]
                    k.mmgroup(psb[pd], td)
                    k.stt(xT[:, o, :T], PS(pd)[:, :T], coef[:, o:o + 1], xT[:, o, :T], ALU.mult, ALU.add,
                          [psb[pd], xTb[o], derb[L]], [xTb[o]])
                    if sq_hook and hf == 1:
                        k.act(xn[:, o, :T], xT[:, o, :T], AF.Square, [xTb[o]], [xnb[o]])

    win_ctr = [0]

    def win_step(S, L, t0, T, groups):
        if not S.is_ctx:
            k.dma(k.sp, cosS[:, :T], cos_d[:, t0:t0 + T], tab_sem, (), [cosb])
            k.dma(k.sp, sinS[:, :T], sin_d[:, t0:t0 + T], tab_sem, (), [sinb])
            k.dma(k.sp, valS[:, :T], valid_d[:, t0:t0 + T], tab_sem, (), [valb])
        QT = cQT_s if S.is_ctx else QT_s[L]
        HH = cHH_s if S.is_ctx else HH_s[L]
        qn_name = "cQT" if S.is_ctx else "QT%d" % L
        ntb = T // 128

        def feat_block(wv, ci):
            pa = win_ctr[0] % 4
            win_ctr[0] += 1
            terms = [(PS(pa)[:, :T], wv[:, kk, ci * 128:(ci + 1) * 128], xn[:, kk, :T], kk == 0, kk == 15,
                      [wcur[1], xnb[kk]]) for kk in range(16)]
            k.mmgroup(psb[pa], terms)
            return pa

        def qk_post(pa, gidx, rope, out_dram, out_sb, out_sb_buf, dname):
            sq, sqb = tb_pool.get()
            k.act(sq[:, :T], PS(pa)[:, :T], AF.Square, [psb[pa]], [sqb])
            k.mmgroup(psb[6], [(PS(6)[:, :T], ones, sq[:, :T], True, True, [sqb, prm2b])])
            sq_, sqb_ = tf_pool.get()
            k.act(sq_[:, :T], PS(6)[:, :T], AF.Ln, [psb[6]], [sqb_], bias=EPS, scale=1.0 / 128.0)
            r, rb = rs_pool.get()
            k.act(r[:, :T], sq_[:, :T], AF.Exp, [sqb_], [rb], scale=-0.5)
            g_ap = qkgE[:, L, gidx:gidx + 1]
            if not rope:
                if out_sb is not None:
                    k.stt(out_sb, PS(pa)[:, :T], g_ap, r[:, :T], ALU.mult, ALU.mult, [psb[pa], rb, prm2b], [out_sb_buf])
                else:
                    qn, qnb = tb_pool.get()
                    k.stt(qn[:, :T], PS(pa)[:, :T], g_ap, r[:, :T], ALU.mult, ALU.mult, [psb[pa], rb, prm2b], [qnb])
                    k.dma(k.sp, out_dram, qn[:, :T], st_sem, [qnb], [sbuf_of(dname)])
                return
            qn, qnb = tb_pool.get()
            k.stt(qn[:, :T], PS(pa)[:, :T], g_ap, r[:, :T], ALU.mult, ALU.mult, [psb[pa], rb, prm2b], [qnb])
            k.mmgroup(psb[7], [(PS(7)[:, :T], perm, qn[:, :T], True, True, [qnb, prm2b])])
            t1, t1b = tf_pool.get()
            k.tt(t1[:, :T], qn[:, :T], cosS[:, :T], ALU.mult, [qnb, cosb], [t1b])
            t2, t2b = tf_pool.get()
            k.tt(t2[:, :T], PS(7)[:, :T], sinS[:, :T], ALU.mult, [psb[7], sinb], [t2b])
            qo, qob = tb_pool.get()
            k.tt(qo[:, :T], t1[:, :T], t2[:, :T], ALU.add, [t1b, t2b], [qob])
            k.dma(k.sp, out_dram, qo[:, :T], st_sem, [qob], [sbuf_of(dname)])

        def v_run(wv, ci0, n, vdram, vcol0, kvslot0, dname):
            for tb_i in range(ntb):
                nn = n * 128
                done = 0
                while done < nn:
                    w_ = min(512, nn - done)
                    pv = 4 + (win_ctr[0] % 2)
                    win_ctr[0] += 1
                    terms = [(PS(pv)[:, :w_], xn[:, kk, tb_i * 128:(tb_i + 1) * 128],
                              wv[:, kk, ci0 * 128 + done:ci0 * 128 + done + w_], kk == 0, kk == 15,
                              [wcur[1], xnb[kk]]) for kk in range(16)]
                    k.mmgroup(psb[pv], terms)
                    if S.is_ctx:
                        nb_ = w_ // 128
                        s0 = kvslot0 + done // 128
                        k.act(cv[:, tb_i, s0:s0 + nb_, :], PS(pv)[:, :w_].rearrange("p (s d) -> p s d", d=128),
                              AF.Copy, [psb[pv]], [cvb])
                    else:
                        vs, vsb = vst_pool.get()
                        k.act(vs[:, :w_], PS(pv)[:, :w_], AF.Copy, [psb[pv]], [vsb])
                        k.dma(k.sp, vdram[t0 + tb_i * 128:t0 + (tb_i + 1) * 128, vcol0 + done:vcol0 + done + w_],
                              vs[:, :w_], st_sem, [vsb], [sbuf_of(dname)])
                    done += w_

        pending = []

        def flush():
            while pending:
                pending.pop(0)()

        for gi in groups:
            w, wb = k.wnext(("win", L, gi))
            wcur = (w, wb)
            wv = _v3(w[:, 0:8192], 512)
            blks = WIN_GROUPS[gi]
            if gi >= 7:
                flush()
                need_cu = not (S.is_ctx and L == 1)
                if not need_cu:
                    continue
                for i in range(2):
                    cc = (gi - 7) * 2 + i
                    pa_a = feat_block(wv, i)
                    pa_g = feat_block(wv, 2 + i)
                    sg, sgb = tf_pool.get()
                    k.act(sg[:, :T], PS(pa_g)[:, :T], AF.Sigmoid, [psb[pa_g]], [sgb])
                    ho, hob = tb_pool.get()
                    if S.is_ctx:
                        k.tt(ho[:, :T], PS(pa_a)[:, :T], sg[:, :T], ALU.mult, [psb[pa_a], sgb], [hob])
                    else:
                        s2, s2b = tf_pool.get()
                        k.tt(s2[:, :T], sg[:, :T], valS[:, :T], ALU.mult, [sgb, valb], [s2b])
                        k.tt(ho[:, :T], PS(pa_a)[:, :T], s2[:, :T], ALU.mult, [psb[pa_a], s2b], [hob])
                    k.dma(k.sp, HH[cc, :, PADC + t0:PADC + t0 + T], ho[:, :T], st_sem, [hob],
                          [sbuf_of("cHH" if S.is_ctx else "HH%d" % L)])
                continue
            ci = 0
            while ci < 4:
                b = blks[ci]
                ctx_kv_only = S.is_ctx and L == 1
                if b < 6 or 10 <= b < 16:
                    if ctx_kv_only:
                        ci += 1
                        continue
                    isA = b < 6
                    qi = b if isA else 6 + (b - 10)
                    pa = feat_block(wv, ci)
                    flush()
                    pending.append(lambda pa=pa, isA=isA, qi=qi: qk_post(pa, 0 if isA else 2, isA and not S.is_ctx,
                                                                        QT[qi, :, t0:t0 + T], None, None, qn_name))
                    ci += 1
                elif b in (6, 7) or 16 <= b < 22:
                    isA = b < 8
                    ki = (b - 6) if isA else 2 + (b - 16)
                    pa = feat_block(wv, ci)
                    flush()
                    if S.is_ctx:
                        pending.append(lambda pa=pa, isA=isA, ki=ki: qk_post(pa, 1 if isA else 3, False, None,
                                                                            ckT[:, ki, :T], ckTb, None))
                    else:
                        pending.append(lambda pa=pa, isA=isA, ki=ki: qk_post(pa, 1 if isA else 3, isA,
                                                                            KT_s[L][ki, :, t0:t0 + T], None, None,
                                                                            "KT%d" % L))
                    ci += 1
                else:
                    n = 0
                    while ci + n < 4 and (blks[ci + n] in (8, 9) or 22 <= blks[ci + n] < 28):
                        n += 1
                    isA = b < 10
                    if isA:
                        v_run(wv, ci, n, VA_s[L], (b - 8) * 128, (b - 8), "VA%d" % L)
                    else:
                        v_run(wv, ci, n, VB_s[L], (b - 22) * 128, 2 + (b - 22), "VB%d" % L)
                    ci += n
        flush()

    att_ctr = [0]
    head_ctr = [0]
    EDGE_TOP = (4, 5)
    EDGE_BOT = (34, 35)

    def offs_of(g, isA_):
        if isA_:
            return [-1, 0, 1]
        if g in EDGE_TOP:
            return [-2, -1, 0, 1, 2, 3]
        if g in EDGE_BOT:
            return [-3, -2, -1, 0, 1, 2]
        return [-2, -1, 0, 1, 2]

    pref = {}

    def attn_prefetch(S, L, t0, T):
        key = (S.is_ctx, L, t0)
        if key in pref:
            return pref[key]
        G = T // 128
        g0 = t0 // 128
        conv_prep(S, L, t0, T)
        rng = {}
        if not S.is_ctx:
            for isA_ in (True, False):
                lo_ = min(g0 + gi_ + offs_of(g0 + gi_, isA_)[0] for gi_ in range(G))
                hi_ = max(g0 + gi_ + offs_of(g0 + gi_, isA_)[-1] for gi_ in range(G)) + 1
                lo_, hi_ = max(0, lo_), min(NCH, hi_)
                rng[isA_] = (lo_, hi_)
            la, ha = rng[True]
            k.dma(k.sp, vA_all[:, :ha - la, :], VA_s[L][la * 128:ha * 128, :].rearrange("(c p) d -> p c d", p=128),
                  vA_sem, [sbuf_of("VA%d" % L)], [vAb])
            lb, hb = rng[False]
            assert hb - lb <= 9 and ha - la <= 6
            k.dma(k.sp, vB_all[:, :hb - lb, :], VB_s[L][lb * 128:hb * 128, :].rearrange("(c p) d -> p c d", p=128),
                  vB_sem, [sbuf_of("VB%d" % L)], [vBb])
        pref[key] = rng
        return rng

    def attn_step(S, L, t0, T, after_head=None, mid_hook=None):
        G = T // 128
        g0 = t0 // 128
        rng = attn_prefetch(S, L, t0, T)
        ckey = (S.is_ctx, L, t0)
        conv_early = conv_done.get(ckey, 0) == 124
        if conv_early:
            conv_finish(S, L, t0, T)
        QT = cQT_s if S.is_ctx else QT_s[L]
        qn_name = "cQT" if S.is_ctx else "QT%d" % L
        for hd in range(12):
            isA = hd < 6
            hh = hd if isA else hd - 6
            kvslot = (hh // 3) if isA else 2 + hh
            q_sem = q_sems[qT_pool.i % 2]
            q, qb = qT_pool.get()
            k.dma(k.sp, q[:, :T], QT[hd, :, t0:t0 + T], q_sem, [sbuf_of(qn_name)], [qb])
            if not S.is_ctx:
                c_lo, c_hi = rng[isA]
                nch = c_hi - c_lo
                k_sem = k_sems[kT_pool.i % 2]
                kt, ktb = kT_pool.get()
                k.dma(k.sp, kt[:, :nch * 128], KT_s[L][kvslot, :, c_lo * 128:c_hi * 128], k_sem,
                      [sbuf_of("KT%d" % L)], [ktb])
            if hd == 2 and mid_hook is not None:
                mid_hook()
            po = 4 + (head_ctr[0] % 2)
            pdn = 6 + (head_ctr[0] % 2)
            head_ctr[0] += 1
            pend_pv = None
            for gi in range(G):
                g = g0 + gi
                if S.is_ctx:
                    offs = []
                elif isA:
                    offs = offs_of(g, True)
                    cls = 1 if g == 4 else (2 if g == 35 else 0)
                else:
                    offs = offs_of(g, False)
                    if g in EDGE_TOP:
                        cls = 1 + EDGE_TOP.index(g)
                    elif g in EDGE_BOT:
                        cls = 3 + EDGE_BOT.index(g)
                    else:
                        cls = 0
                    bb, bbb = bB_pool.items[bB_pool.i % 2]
                    bsem = bB_sem[bB_pool.i % 2]
                    bB_pool.i += 1
                    nof = len(offs)
                    k.dma(k.pool, bb[:, :nof * 128].rearrange("p (o q) -> p o q", q=128),
                          biasB_d[L, cls, hh, :, offs[0] + 3:offs[0] + 3 + nof, :], bsem, (), [bbb])
                n_loc = len(offs)
                n = n_loc + 2
                pp = (att_ctr[0] % 2) * 2
                att_ctr[0] += 1
                qg = q[:, gi * 128:(gi + 1) * 128]

                def S_ap(j):
                    return PS(pp + j // 4)[:, (j % 4) * 128:(j % 4 + 1) * 128]

                def S_buf(j):
                    return psb[pp + j // 4]

                for bank in range(2):
                    js = [j for j in range(n) if j // 4 == bank]
                    if not js:
                        continue
                    terms = []
                    for j in js:
                        if j < n_loc:
                            c = g + offs[j] - c_lo
                            bias_ap = bAS[:, cls, j, :] if isA else bb[:, j * 128:(j + 1) * 128]
                            bias_buf = prmb if isA else bbb
                            terms.append((S_ap(j), kt[:, c * 128:(c + 1) * 128], qg, True, False, [ktb, qb]))
                            terms.append((S_ap(j), ident, bias_ap, False, True, [bias_buf, prm2b]))
                        else:
                            cc = j - n_loc
                            terms.append((S_ap(j), ckT[:, kvslot, cc * 128:(cc + 1) * 128], qg, True, True, [ckTb, qb]))
                    k.mmgroup(psb[pp + bank], terms)
                pT, pTb = pT_pool.get()
                for bank in range(2):
                    js = [j for j in range(n) if j // 4 == bank]
                    if not js:
                        continue
                    m = len(js)
                    k.act(pT[:, bank * 512:bank * 512 + m * 128], PS(pp + bank)[:, 0:m * 128], AF.Exp,
                          [psb[pp + bank]], [pTb])
                to, tdn = [], []
                for j in range(n):
                    pj = pT[:, j * 128:(j + 1) * 128]
                    if j < n_loc:
                        c = g + offs[j] - c_lo
                        if isA:
                            lv = vA_all[:, c, (hh // 3) * 128:(hh // 3 + 1) * 128]
                            lvb = vAb
                        else:
                            lv = vB_all[:, c, hh * 128:(hh + 1) * 128]
                            lvb = vBb
                    else:
                        lv = cv[:, j - n_loc, kvslot, :]
                        lvb = cvb
                    to.append((PS(po)[:, gi * 128:(gi + 1) * 128], lv, pj, j == 0, j == n - 1, [lvb, pTb]))
                    tdn.append((PS(pdn)[:, gi * 128:(gi + 1) * 128], ones, pj, j == 0, j == n - 1, [pTb, prm2b]))
                if pend_pv is not None:
                    k.mmgroup(psb[po], pend_pv[0])
                    k.mmgroup(psb[pdn], pend_pv[1])
                pend_pv = (to, tdn)
            k.mmgroup(psb[po], pend_pv[0])
            k.mmgroup(psb[pdn], pend_pv[1])
            r, rb = rs_pool.get()
            dn_, dnb_ = tf_pool.get()
            if isA:
                k.act(dn_[:, :T], PS(pdn)[:, :T], AF.Ln, [psb[pdn], esinkb], [dnb_], bias=esink[:, L, hh:hh + 1])
            else:
                k.act(dn_[:, :T], PS(pdn)[:, :T], AF.Ln, [psb[pdn]], [dnb_])
            k.act(r[:, :T], dn_[:, :T], AF.Exp, [dnb_], [rb], scale=-1.0)
            k.tt(xn[:, hd, :T], PS(po)[:, :T], r[:, :T], ALU.mult, [psb[po], rb], [xnb[hd]])
            if not conv_early:
                conv_taps_n(S, L, t0, T, 11)
        if not conv_early:
            conv_finish(S, L, t0, T)

    def conv_prep(S, L, t0, T):
        HH = cHH_s if S.is_ctx else HH_s[L]
        hname = "cHH" if S.is_ctx else "HH%d" % L
        k.dma(k.sp, hhc[:, :, :T + 30], HH[:, :, PADC + t0 - 15:PADC + t0 + T + 15].rearrange("c p t -> p c t"),
              hhc_sem, [sbuf_of(hname), sbuf_of("pads")], [hhcb])

    conv_done = {}

    def conv_taps_n(S, L, t0, T, n):
        key = (S.is_ctx, L, t0)
        d0 = conv_done.get(key, 0)
        for idx in range(d0, min(d0 + n, 124)):
            cc, j = idx // 31, idx % 31
            if j == 0:
                k.ts(acc[cc][:, :T], hhc[:, cc, 0:T], dwS[:, L, cc, 0:1], cpS[:, L, 0, cc:cc + 1], ALU.mult, ALU.add,
                     [hhcb, prmb], [accb[cc]])
            else:
                k.stt(acc[cc][:, :T], hhc[:, cc, j:j + T], dwS[:, L, cc, j:j + 1], acc[cc][:, :T], ALU.mult, ALU.add,
                      [hhcb, prmb, accb[cc]], [accb[cc]])
        conv_done[key] = min(d0 + n, 124)

    def conv_finish(S, L, t0, T):
        assert conv_done.get((S.is_ctx, L, t0), 0) == 124
        for cc in range(4):
            k.act(hT[:, cc, :T], acc[cc][:, :T], AF.Copy, [accb[cc]], [hTb[cc]])
            k.act(hT[:, 4 + cc, :T], acc[cc][:, :T], AF.Square, [accb[cc]], [hTb[4 + cc]])
        k.mmgroup(psb[6], [(PS(6)[:, :T], ones, hT[:, cc, :T], cc == 0, cc == 3, [hTb[cc], prm2b]) for cc in range(4)])
        k.mmgroup(psb[7], [(PS(7)[:, :T], ones, hT[:, 4 + cc, :T], cc == 0, cc == 3, [hTb[4 + cc], prm2b]) for cc in range(4)])
        m, mb = rs_pool.get()
        k.ts(m[:, :T], PS(6)[:, :T], 1.0 / 512.0, None, ALU.mult, None, [psb[6]], [mb])
        msq, msqb = tf_pool.get()
        k.tt(msq[:, :T], m[:, :T], m[:, :T], ALU.mult, [mb], [msqb])
        var, varb = tf_pool.get()
        k.stt(var[:, :T], PS(7)[:, :T], 1.0 / 512.0, msq[:, :T], ALU.mult, ALU.subtract, [psb[7], msqb], [varb])
        sd_, sdb_ = tf_pool.get()
        k.act(sd_[:, :T], var[:, :T], AF.Ln, [varb], [sdb_], bias=EPS, scale=1.0)
        r, rb = rs_pool.get()
        k.act(r[:, :T], sd_[:, :T], AF.Exp, [sdb_], [rb], scale=-0.5)
        for cc in range(4):
            z, zb = tf_pool.get()
            k.tt(z[:, :T], acc[cc][:, :T], m[:, :T], ALU.subtract, [accb[cc], mb], [zb])
            z2, z2b = tf_pool.get()
            k.tt(z2[:, :T], z[:, :T], r[:, :T], ALU.mult, [zb, rb], [z2b])
            k.act(xn[:, 12 + cc, :T], z2[:, :T], AF.Silu, [z2b, prmb], [xnb[12 + cc]],
                  bias=cpS[:, L, 2, cc:cc + 1], scale=cpS[:, L, 1, cc:cc + 1])

    wo_ctr = [0]

    def wout_step(S, L, T, sq_hook=False):
        coef = der[L][:, S.col, 5, :]
        for g in range(4):
            w, wb = k.wnext(("wout", L, g))
            wv = _v3(w[:, 0:8192], 512)
            for s in range(4):
                o = g * 4 + s
                pd = wo_ctr[0] % 4
                wo_ctr[0] += 1
                terms = [(PS(pd)[:, :T], wv[:, kk, s * 128:(s + 1) * 128], xn[:, kk, :T], kk == 0, kk == 15,
                          [wb, xnb[kk]]) for kk in range(16)]
                k.mmgroup(psb[pd], terms)
                k.stt(xT[:, o, :T], PS(pd)[:, :T], coef[:, o:o + 1], xT[:, o, :T], ALU.mult, ALU.add,
                      [psb[pd], xTb[o], derb[L]], [xTb[o]])
                if sq_hook:
                    k.act(hT[:, o, :T], xT[:, o, :T], AF.Square, [xTb[o]], [hTb[o]])

    def load_x(src2d, T, rname=None):
        rd = [sbuf_of(rname)] if rname else []
        for h in range(2):
            k.dma(k.sp, xT[:, 8 * h:8 * h + 8, :T],
                  src2d[1024 * h:1024 * (h + 1), :].rearrange("(k p) t -> p k t", p=128), x_sem, rd, xTb[8 * h:8 * h + 8])

    def store_x(dst2d, T, sem, wname=None):
        wr = [sbuf_of(wname)] if wname else []
        for h in range(2):
            k.dma(k.sp, dst2d[1024 * h:1024 * (h + 1), :].rearrange("(k p) t -> p k t", p=128),
                  xT[:, 8 * h:8 * h + 8, :T], sem, xTb[8 * h:8 * h + 8], wr)

    mod_step(0)
    for i in range(NT1 if debug != 1 else 2):
        t0 = i * TT
        load_x(xT_d[:, t0:t0 + TT], TT)
        norm_step(lat, 0, TT, 0)
        ffn_step(lat, 0, 0, TT, sq_hook=True, mod_inter=True)
        store_x(x1_s[0][:, t0:t0 + TT], TT, st_sem, "x1_0")
        norm_step(lat, 0, TT, 1, presq="xn")
        win_step(lat, 0, t0, TT, ALLG)
        if i == 0:
            k.preconv_on = PRECONV
            load_x(ctxT_d, CTX)
            norm_step(ctxs, 0, CTX, 0)
            ffn_step(ctxs, 0, 0, CTX, sq_hook=True)
            store_x(cx1_s, CTX, st_sem, "cx1")
            norm_step(ctxs, 0, CTX, 1, presq="xn")
            win_step(ctxs, 0, 0, CTX, ALLG)
    if debug:
        dbg_d = nc.dram_tensor("dbg_der", [128, 2, 2 * 9 * 16], F32, kind="ExternalOutput").ap()
        for l in range(2):
            k.dma(k.sp, dbg_d[:, l, :], der[l][:].rearrange("p a b c -> p (a b c)"), out_sem, [derb[l]], [])
        dbg_m = nc.dram_tensor("dbg_mod", [128, 2, 288], F32, kind="ExternalOutput").ap()
        for l in range(2):
            k.dma(k.sp, dbg_m[:, l, :], modT[l][:].rearrange("p a b -> p (a b)"), out_sem, [modb[l]], [])
        dbg_c = nc.dram_tensor("dbg_ck", [128, 8 * CTX], BF16, kind="ExternalOutput").ap()
        k.dma(k.sp, dbg_c, ckT[:].rearrange("p a b -> p (a b)"), out_sem, [ckTb], [])
        dbg_v = nc.dram_tensor("dbg_cv", [128, 2 * 8 * 128], BF16, kind="ExternalOutput").ap()
        k.dma(k.sp, dbg_v, cv[:].rearrange("p a b c -> p (a b c)"), out_sem, [cvb], [])
    def ctx_phase2():
        load_x(cx1_s, CTX, "cx1")
        attn_step(ctxs, 0, 0, CTX)
        wout_step(ctxs, 0, CTX, sq_hook=True)
        norm_step(ctxs, 0, CTX, 2, presq="hT")
        ffn_step(ctxs, 0, 1, CTX, sq_hook=True)
        norm_step(ctxs, 1, CTX, 0, presq="xn")
        ffn_step(ctxs, 1, 0, CTX)
        store_x(cx1_s, CTX, st_sem, "cx1")

    def rest_of_program():
        for i in range(NT2):
            t0 = 256 + i * TT
            attn_step(lat, 0, t0, TT, mid_hook=lambda t0=t0: load_x(x1_s[0][:, t0:t0 + TT], TT, "x1_0"))
            wout_step(lat, 0, TT, sq_hook=True)
            th = None
            if 0 < i < NT2 - 1:
                attn_prefetch(lat, 0, t0 + TT, TT)
                th = (lambda t0=t0: conv_taps_n(lat, 0, t0 + TT, TT, 3))
            norm_step(lat, 0, TT, 2, presq="hT")
            ffn_step(lat, 0, 1, TT, sq_hook=True, tap_hook=th)
            norm_step(lat, 1, TT, 0, presq="xn")
            ffn_step(lat, 1, 0, TT, sq_hook=True)
            store_x(x1_s[1][:, t0:t0 + TT], TT, st_sem, "x1_1")
            norm_step(lat, 1, TT, 1, presq="xn")
            win_step(lat, 1, t0, TT, ALLG)
            if i == 0:
                ctx_phase2()
        load_x(cx1_s, CTX, "cx1")
        norm_step(ctxs, 1, CTX, 1)
        win_step(ctxs, 1, 0, CTX, KVG)
        for i in range(NT3):
            t0 = HALO + i * TT
            attn_step(lat, 1, t0, TT, mid_hook=lambda t0=t0: load_x(x1_s[1][:, t0:t0 + TT], TT, "x1_1"))
            wout_step(lat, 1, TT, sq_hook=True)
            th = None
            if i < NT3 - 1:
                attn_prefetch(lat, 1, t0 + TT, TT)
                th = (lambda t0=t0: conv_taps_n(lat, 1, t0 + TT, TT, 3))
            norm_step(lat, 1, TT, 2, presq="hT")
            ffn_step(lat, 1, 1, TT, tap_hook=th)
            store_x(out_d[:, t0 - HALO:t0 - HALO + TT], TT, out_sem)

    if debug != 1:
        rest_of_program()
    for i_ in range(3):
        k.pool.prog.append(("w", k.ws_sem[i_].h, k.ws_sem[i_].total))
    k.pool.prog.append(("w", k.wst_sem.h, k.wst_sem.total))
    k.sp.prog.append(("w", out_sem.h, out_sem.total))
    k.sp.prog.append(("w", st_sem.h, st_sem.total))

    def emit(eng_obj, prog):
        for it in prog:
            if it[0] == "w":
                eng_obj.wait_ge(it[1], it[2])
            else:
                ins = it[1](eng_obj)
                if it[2] is not None:
                    ins.then_inc(it[2], it[3])

    with nc.Block() as block:
        @block.tensor
        def _(e):
            emit(e, k.pe.prog)

        @block.scalar
        def _(e):
            emit(e, k.act_e.prog)

        @block.vector
        def _(e):
            emit(e, k.dve.prog)

        @block.gpsimd
        def _(e):
            emit(e, k.pool.prog)

        @block.sync
        def _(e):
            emit(e, k.sp.prog)
    k.stats = {n: len(e.prog) for n, e in (("pe", k.pe), ("act", k.act_e), ("dve", k.dve), ("pool", k.pool), ("sp", k.sp))}
    print("program sizes", k.stats, "nsem", k.nsem, "sbuf_free", nc.sbuf_bytes_remaining)


def _rope_tables(t_ext):
    row = (np.clip(t_ext, 0, SEQ - 1) // 64).astype(np.float32)
    col = (np.clip(t_ext, 0, SEQ - 1) % 64).astype(np.float32)
    n_freq = 32
    inv_freq = (np.float32(10000.0) ** (-np.arange(n_freq, dtype=np.float32) / np.float32(n_freq))).astype(np.float32)
    p = np.arange(128)
    f = p % 32
    half = (p % 64) // 32
    pos = np.where((p < 64)[:, None], row[None, :], col[None, :]).astype(np.float32)
    ang = (pos * inv_freq[f][:, None]).astype(np.float32)
    cos = np.cos(ang).astype(np.float32)
    sin = np.sin(ang).astype(np.float32) * np.where(half == 0, -1.0, 1.0).astype(np.float32)[:, None]
    return cos, sin.astype(np.float32)


def _consts():
    c = np.zeros((128, 3, 128), np.float32)
    c[:, 0, :] = 1.0
    c[:, 1, :] = np.eye(128, dtype=np.float32)
    p = np.arange(128)
    half = (p % 64) // 32
    pm = np.where(half == 0, p + 32, p - 32)
    c[pm, 2, p] = 1.0
    return c


def _biasA(j):
    kk = np.arange(128)[:, None]
    qq = np.arange(128)[None, :]
    t = np.zeros((128, 3, 3, 128), np.float32)
    for cls in range(3):
        t[:, cls, 0, :] = np.where(kk >= qq, 0.0, NEG)
        t[:, cls, 1, :] = 0.0
        t[:, cls, 2, :] = np.where(kk <= qq, 0.0, NEG)
    if j == 0:
        t[:, 1, 0, :] = NEG
    if j == 3:
        t[:, 2, 2, :] = NEG
    return t


def _biasB(rpb, j):
    out = np.full((2, 5, 6, 128, 7, 128), NEG, np.float32)
    kr = (np.arange(128) // 64)[:, None, None]
    kc = (np.arange(128) % 64)[:, None, None]
    oo = (np.arange(7) - 3)[None, :, None]
    qr = (np.arange(128) // 64)[None, None, :]
    qc = (np.arange(128) % 64)[None, None, :]
    drow = 2 * oo + kr - qr
    dcol = kc - qc
    cs = np.clip(qc - 8, 0, 48)
    col_ok = (kc >= cs) & (kc <= cs + 15)
    for cls in range(5):
        lo = -4 * np.ones_like(qr)
        if cls in (1, 2) and j == 0:
            r = (cls - 1) * 2 + qr
            lo = -r
        elif cls in (3, 4) and j == 3:
            r = 252 + (cls - 3) * 2 + qr
            lo = 248 - r
        ok = (drow >= lo) & (drow <= lo + 7) & col_ok
        di = np.clip(drow + 7, 0, 14)
        dj = np.clip(dcol + 15, 0, 30)
        di_b, dj_b = np.broadcast_arrays(di, dj)
        for l in range(2):
            for h in range(6):
                vals = rpb[l, h][di_b, dj_b]
                out[l, cls, h] = np.where(ok, vals, np.float32(NEG))
    return out


def prep_core_inputs(c, inp):
    b, j = c // 4, c % 4
    s0 = j * OWN - HALO
    t_ext = s0 + np.arange(NEXT)
    valid = (t_ext >= 0) & (t_ext < SEQ)
    xT = np.zeros((D, NEXT), np.float32)
    lo, hi = max(s0, 0), min(s0 + NEXT, SEQ)
    xT[:, lo - s0:hi - s0] = inp["x"][b, lo:hi, :].T
    m = {}
    m["xT"] = xT
    m["ctxT"] = np.ascontiguousarray(inp["ctx"][b].T)
    cm = np.zeros((128, 16, 2), np.float32)
    cm[:, :, 0] = inp["c"][b].reshape(16, 128).T
    cm[:, :, 1] = inp["c_ctx"].reshape(16, 128).T
    m["cm"] = cm
    m["w_mod"] = inp["w_mod"]
    m["b_modT"] = np.ascontiguousarray(inp["b_mod"].reshape(2, 144, 128).transpose(0, 2, 1))
    gn = np.stack([inp["norm_ffn1"], inp["norm_mix"], inp["norm_ffn2"]], axis=1)
    m["gnT"] = np.ascontiguousarray(gn.reshape(2, 3, 16, 128).transpose(3, 0, 1, 2))
    for n in ("ffn1_w_gate", "ffn1_w_up", "ffn1_w_down", "ffn2_w_gate", "ffn2_w_up", "ffn2_w_down", "w_in", "w_out"):
        m[n] = inp[n]
    qkg = np.stack([inp["a_q_norm"], inp["a_k_norm"], inp["b_q_norm"], inp["b_k_norm"]], axis=-1)
    m["qkg"] = np.ascontiguousarray(qkg.transpose(1, 0, 2))
    m["sinkT"] = np.ascontiguousarray(np.broadcast_to(inp["a_sink"][None], (128, 2, 6)))
    m["biasA"] = _biasA(j)
    m["biasB"] = _biasB(inp["b_rpb"], j)
    cos, sin = _rope_tables(t_ext)
    m["cosT"] = cos
    m["sinT"] = sin
    m["validT"] = np.ascontiguousarray(np.broadcast_to(valid.astype(np.float32)[None], (128, NEXT)))
    m["dwT"] = np.ascontiguousarray(inp["c_dw_w"].reshape(2, 31, 4, 128).transpose(3, 0, 2, 1))
    cp = np.stack([inp["c_dw_b"], inp["c_ln_g"], inp["c_ln_b"]], axis=1)
    m["cpT"] = np.ascontiguousarray(cp.reshape(2, 3, 4, 128).transpose(3, 0, 1, 2))
    m["consts"] = _consts()
    return m


_NC_CACHE = {}


def kernel(**inputs):
    inp = {k_: np.asarray(v) for k_, v in inputs.items()}
    if "nc" not in _NC_CACHE:
        _NC_CACHE["nc"] = build_nc()
    nc = _NC_CACHE["nc"]
    in_maps = [prep_core_inputs(c, inp) for c in range(NCORE)]
    res = run_bass_kernel_spmd(nc, in_maps, core_ids=list(range(NCORE)))
    out = np.empty((2, SEQ, D), np.float32)
    for c in range(NCORE):
        b, j = c // 4, c % 4
        out[b, j * OWN:(j + 1) * OWN, :] = res.results[c]["outT"].T
    return out
```
